# Optimizing a Trainium2 kernel written in Bass

```python
import jax, jax.numpy as jnp
from jax import lax
import numpy as np


D_MODEL = 2048
BATCH = 1
SEQ = 16384
DEPTH = 2
DEC_BATCH = 4
DEC_SEQ = 8192
PAST_LEN = 128

GLA_HEADS = 4
GLA_WIDTH = D_MODEL // 2
GLA_DV = GLA_WIDTH // GLA_HEADS
GLA_DK = GLA_DV // 2
GLA_KWIDTH = GLA_HEADS * GLA_DK
GLA_RANK = 16
GLA_TAU = 16.0
HG_DIM = 128
HG_WIDTH = D_MODEL - GLA_WIDTH
HG_HEADS = HG_WIDTH // HG_DIM
MIX_WIDTH = GLA_WIDTH + HG_WIDTH
D_FF = 4 * D_MODEL
CHUNK = 64
LN_EPS = 1e-5
RMS_EPS = 1e-6
ALPHA = (2.0 * DEPTH) ** 0.25
BETA = (8.0 * DEPTH) ** -0.25
SPLIT_WIDTHS = (GLA_KWIDTH, GLA_KWIDTH, GLA_WIDTH, GLA_WIDTH, 2 * GLA_RANK,
                HG_WIDTH, HG_WIDTH, HG_WIDTH, HG_WIDTH, HG_WIDTH)
IN_WIDTH = sum(SPLIT_WIDTHS)
SPLIT_IDX = tuple(sum(SPLIT_WIDTHS[:i + 1]) for i in range(len(SPLIT_WIDTHS) - 1))

kernel_name = 'hymba_gla_hgrn2_bidir_encoder'


def _heads(a, n_heads):
    B, T, W = a.shape
    return a.reshape(B, T, n_heads, W // n_heads).transpose(0, 2, 1, 3)


def _merge(a):
    B, H, T, d = a.shape
    return a.transpose(0, 2, 1, 3).reshape(B, T, H * d)


def _rmsnorm(a, g):
    a = a.astype(jnp.float32)
    return a * lax.rsqrt(jnp.mean(a * a, axis=-1, keepdims=True) + RMS_EPS) * g.astype(jnp.float32)


def _layernorm(a, g, b):
    a = a.astype(jnp.float32)
    mu = jnp.mean(a, axis=-1, keepdims=True)
    var = jnp.mean(jnp.square(a - mu), axis=-1, keepdims=True)
    return (a - mu) * lax.rsqrt(var + LN_EPS) * g.astype(jnp.float32) + b.astype(jnp.float32)


def _chunk_gated_linear(q, k, v, g):
    B, H, T, dk = q.shape
    dv = v.shape[-1]
    n = T // CHUNK

    def to_chunks(a):
        return a.reshape(B, H, n, CHUNK, a.shape[-1]).transpose(2, 0, 1, 3, 4)

    mask = jnp.tril(jnp.ones((CHUNK, CHUNK), dtype=bool))[:, :, None]

    def step(S, inp):
        qc, kc, vc, gc = inp
        b = jnp.cumsum(gc, axis=2)
        o_inter = jnp.einsum('bhid,bhde->bhie', qc * jnp.exp(b), S)
        diff = b[:, :, :, None, :] - b[:, :, None, :, :]
        decay = jnp.where(mask, jnp.exp(jnp.where(mask, diff, 0.0)), 0.0)
        scores = jnp.einsum('bhid,bhjd,bhijd->bhij', qc, kc, decay)
        o = o_inter + jnp.einsum('bhij,bhje->bhie', scores, vc)
        b_last = b[:, :, -1:, :]
        S = jnp.exp(b_last[:, :, 0, :])[..., None] * S + jnp.einsum(
            'bhjd,bhje->bhde', kc * jnp.exp(b_last - b), vc)
        return S, o

    S0 = jnp.zeros((B, H, dk, dv), jnp.float32)
    _, o = lax.scan(step, S0, (to_chunks(q), to_chunks(k), to_chunks(v), to_chunks(g)))
    return o.transpose(1, 2, 0, 3, 4).reshape(B, H, T, dv)


def _bidirectional(q, k_f, k_b, v, g_f, g_b):
    flip = lambda a: jnp.flip(a, axis=2)
    fwd = _chunk_gated_linear(q, k_f, v, g_f)
    bwd = flip(_chunk_gated_linear(flip(q), flip(k_b), flip(v), flip(g_b)))
    return fwd + bwd


def _hgrn_gate(z, lb):
    log_f = jnp.logaddexp(jnp.log(lb), jnp.log1p(-lb) + jax.nn.log_sigmoid(z))
    one_minus_f = (1.0 - lb) * jax.nn.sigmoid(-z)
    return log_f, one_minus_f


def _lower_bounds(p):
    c = jnp.cumsum(jax.nn.softmax(p.astype(jnp.float32), axis=0), axis=0)
    return c - c[0:1]


def _mixer(x, w_in, w_lr2, b_lr, gla_g, hg_g, lb_f, lb_b, w_out):
    z = jnp.einsum('btd,de->bte', x, w_in).astype(jnp.float32)
    gq, gk, gv, gog, glr, hq, hff, hfb, hi, hog = jnp.split(z, SPLIT_IDX, axis=-1)
    w_lr2 = w_lr2.astype(jnp.float32)
    b_lr = b_lr.astype(jnp.float32)
    ga_f = jax.nn.log_sigmoid(glr[..., :GLA_RANK] @ w_lr2[0] + b_lr[0]) / GLA_TAU
    ga_b = jax.nn.log_sigmoid(glr[..., GLA_RANK:] @ w_lr2[1] + b_lr[1]) / GLA_TAU
    q = _heads(gq * (GLA_DK ** -0.5), GLA_HEADS)
    k = _heads(gk, GLA_HEADS)
    v = _heads(gv, GLA_HEADS)
    o_gla = _bidirectional(q, k, k, v, _heads(ga_f, GLA_HEADS), _heads(ga_b, GLA_HEADS))
    o_gla = _merge(_rmsnorm(o_gla, gla_g)) * jax.nn.silu(gog)
    logf_f, kf = _hgrn_gate(hff, lb_f)
    logf_b, kb = _hgrn_gate(hfb, lb_b)
    o_hg = _bidirectional(_heads(hq, HG_HEADS), _heads(kf, HG_HEADS), _heads(kb, HG_HEADS),
                          _heads(hi, HG_HEADS), _heads(logf_f, HG_HEADS), _heads(logf_b, HG_HEADS))
    o_hg = _merge(_rmsnorm(o_hg, hg_g)) * jax.nn.silu(hog)
    o = jnp.concatenate([o_gla, o_hg], axis=-1)
    return jnp.einsum('bte,ed->btd', o, w_out)


def _mlp(x, w_up, w_down):
    h = jnp.square(jax.nn.relu(jnp.einsum('btd,df->btf', x, w_up)))
    return jnp.einsum('btf,fd->btd', h, w_down)


def _trunk(x, w_in, gla_w_lr2, gla_b_lr, gla_norm_g, hg_norm_g, lbs_f, lbs_b,
           w_out, ln1_g, ln1_b, w_up, w_down, ln2_g, ln2_b):
    dt = x.dtype
    h = x.astype(jnp.float32)
    for l in range(DEPTH):
        m = _mixer(h, w_in[l], gla_w_lr2[l], gla_b_lr[l], gla_norm_g[l], hg_norm_g[l],
                   lbs_f[l], lbs_b[l], w_out[l])
        h = _layernorm(ALPHA * h + m, ln1_g[l], ln1_b[l])
        f = _mlp(h, w_up[l], w_down[l])
        h = _layernorm(ALPHA * h + f, ln2_g[l], ln2_b[l])
    return h.astype(dt)


def setup_inputs(seed: int = 0) -> dict:
    key = jax.random.key(seed)
    ks = jax.random.split(key, 16)
    f32 = jnp.float32
    nrm = lambda k, s, sc: jax.random.normal(k, s, f32) * sc
    return {
        'x_prompt': nrm(ks[0], (BATCH, SEQ, D_MODEL), 1.0),
        'x_sample': nrm(ks[1], (DEC_BATCH, DEC_SEQ, D_MODEL), 1.0),
        'w_in': nrm(ks[2], (DEPTH, D_MODEL, IN_WIDTH), D_MODEL ** -0.5),
        'gla_w_lr2': nrm(ks[3], (DEPTH, 2, GLA_RANK, GLA_KWIDTH), GLA_RANK ** -0.5),
        'gla_b_lr': nrm(ks[4], (DEPTH, 2, GLA_KWIDTH), 0.1),
        'gla_norm_g': 1.0 + nrm(ks[5], (DEPTH, GLA_DV), 0.02),
        'hg_norm_g': 1.0 + nrm(ks[6], (DEPTH, HG_DIM), 0.02),
        'lower_bounds': nrm(ks[7], (2, DEPTH, HG_WIDTH), 0.1),
        'w_out': nrm(ks[8], (DEPTH, MIX_WIDTH, D_MODEL), BETA * MIX_WIDTH ** -0.5),
        'ln1_g': 1.0 + nrm(ks[9], (DEPTH, D_MODEL), 0.02),
        'ln1_b': nrm(ks[10], (DEPTH, D_MODEL), 0.02),
        'w_up': nrm(ks[11], (DEPTH, D_MODEL, D_FF), D_MODEL ** -0.5),
        'w_down': nrm(ks[12], (DEPTH, D_FF, D_MODEL), BETA * D_FF ** -0.5),
        'ln2_g': 1.0 + nrm(ks[13], (DEPTH, D_MODEL), 0.02),
        'ln2_b': nrm(ks[14], (DEPTH, D_MODEL), 0.02),
    }


def reference(x_prompt, x_sample, w_in, gla_w_lr2, gla_b_lr, gla_norm_g, hg_norm_g,
              lower_bounds, w_out, ln1_g, ln1_b, w_up, w_down, ln2_g, ln2_b):
    lbs_f = _lower_bounds(lower_bounds[0])
    lbs_b = _lower_bounds(lower_bounds[1])
    y_prompt = _trunk(x_prompt, w_in, gla_w_lr2, gla_b_lr, gla_norm_g, hg_norm_g, lbs_f, lbs_b,
                      w_out, ln1_g, ln1_b, w_up, w_down, ln2_g, ln2_b)
    y_sample = _trunk(x_sample, w_in, gla_w_lr2, gla_b_lr, gla_norm_g, hg_norm_g, lbs_f, lbs_b,
                      w_out, ln1_g, ln1_b, w_up, w_down, ln2_g, ln2_b)
    return (y_prompt, y_sample)
```

```python
import numpy as np
from contextlib import ExitStack
import concourse.bass as bass
import concourse.mybir as mybir
from concourse.bass_utils import run_bass_kernel_spmd

F32 = mybir.dt.float32
BF16 = mybir.dt.bfloat16
AF = mybir.ActivationFunctionType
ALU = mybir.AluOpType
AX = mybir.AxisListType

D = 2048
DIN = 8224
DFF = 8192
L = 2
ALPHA = (2.0 * L) ** 0.25
LN_EPS = 1e-5
RMS_EPS = 1e-6
DK = 128
C_GQ, C_GK, C_GV, C_GOG, C_GLR, C_HQ, C_HFF, C_HFB, C_HI, C_HOG = 0, 512, 1024, 2048, 3072, 3104, 4128, 5152, 6176, 7200
NBLK = 55
DBG_OUT = ("opf", "vbuf", "gbuf", "slf", "slb", "slend", "h1buf")
FULL_SEGM = [4, 2, 2, 2, 2]
SAME_DIST = 6


class R:
    __slots__ = ("n", "w", "rs", "excl")

    def __init__(self, n, excl=False):
        self.n = n
        self.w = None
        self.rs = []
        self.excl = excl


class O:
    __slots__ = ("e", "fn", "deps", "sig", "val", "i", "dma", "inc")


class Prog:
    ENG = ("pe", "act", "dve", "pool", "sp")

    def __init__(self):
        self.st = {k: [] for k in self.ENG}
        self.bar = {k: None for k in self.ENG}
        self.dma_last = {}

    def op(self, e, fn, rd=(), wr=(), dma=None, inc=16):
        o = O()
        o.e, o.fn, o.dma, o.inc, o.sig, o.val = e, fn, dma, inc, False, 0
        ex = [r for r in rd if r.excl]
        if ex:
            rd = [r for r in rd if not r.excl]
            wr = list(wr) + [r for r in ex if r not in wr]
        d = []
        for r in rd:
            if r.w is not None:
                d.append(r.w)
        for w in wr:
            if w.w is not None:
                d.append(w.w)
            d.extend(w.rs)
        if self.bar[e] is not None:
            d.extend(self.bar[e])
            self.bar[e] = None
        o.deps = self._reduce(d)
        for r in rd:
            r.rs.append(o)
        for w in wr:
            w.w = o
            w.rs = []
        o.i = len(self.st[e])
        self.st[e].append(o)
        if dma is not None:
            self.dma_last[dma] = o
        return o

    @staticmethod
    def _reduce(d):
        best = {}
        for p in d:
            k = ("d", p.dma) if p.dma is not None else ("e", p.e)
            q = best.get(k)
            if q is None or (p.i > q.i if p.dma is None else p.i > q.i):
                best[k] = p
        return list(best.values())

    def barrier(self):
        last = []
        for k in self.ENG:
            for o in reversed(self.st[k]):
                if o.dma is None:
                    last.append(o)
                    break
        last.extend(self.dma_last.values())
        for k in self.ENG:
            self.bar[k] = list(last)

    def finalize(self):
        for e, lst in self.st.items():
            for o in lst:
                keep = []
                for p in o.deps:
                    if p.dma is None and p.e == e and (e == "pe" or (e != "pool" and o.i - p.i > SAME_DIST)):
                        continue
                    if p.dma is None:
                        p.sig = True
                    keep.append(p)
                o.deps = keep
        cnt = {}
        for e, lst in self.st.items():
            c = 0
            for o in lst:
                if o.dma is not None:
                    cnt[o.dma] = cnt.get(o.dma, 0) + o.inc
                    o.val = cnt[o.dma]
                elif o.sig:
                    c += 1
                    o.val = c
        self.dma_tot = cnt

    def emit(self, e, eng, esem, dsem):
        waited = {}
        for o in self.st[e]:
            need = {}
            for p in o.deps:
                k = ("d", p.dma) if p.dma is not None else ("e", p.e)
                if need.get(k, 0) < p.val:
                    need[k] = p.val
            for k, v in need.items():
                if waited.get(k, 0) < v:
                    eng.wait_ge(dsem[k[1]] if k[0] == "d" else esem[k[1]], v)
                    waited[k] = v
            ins = o.fn(eng)
            if o.dma is not None:
                ins.then_inc(dsem[o.dma], o.inc)
            elif o.sig:
                ins.then_inc(esem[e], 1)
        if e == "sp":
            for k, v in self.dma_tot.items():
                if waited.get(("d", k), 0) < v:
                    eng.wait_ge(dsem[k], v)


class Rot:
    def __init__(self, items):
        self.items = list(items)
        self.i = 0

    def next(self):
        x = self.items[self.i % len(self.items)]
        self.i += 1
        return x


def block_pieces():
    pcs = {}
    for b in range(2):
        pl = []
        for i in range(2):
            h = 2 * b + i
            pl.append(("w_in", 0, C_GQ + h * 128, 128, i * 256))
            pl.append(("w_in", 0, C_GK + h * 128, 128, i * 256 + 128))
        pcs[b] = pl
    for h in range(8):
        pcs[2 + h] = [("w_in", 0, C_HQ + h * 128, 128, 0), ("w_in", 0, C_HFF + h * 128, 128, 128),
                      ("w_in", 0, C_HFB + h * 128, 128, 256)]
    pcs[10] = [("w_in", 0, C_GLR, 32, 0)]
    for i, c in enumerate((C_GV, C_GV + 512, C_HI, C_HI + 512, C_GOG, C_GOG + 512, C_HOG, C_HOG + 512)):
        pcs[11 + i] = [("w_in", 0, c, 512, 0)]
    for cb in range(4):
        pcs[19 + cb] = [("w_out", 0, cb * 512, 512, 0)]
    for ub in range(16):
        pcs[23 + ub] = [("w_up", 0, ub * 512, 512, 0)]
    for cb in range(4):
        for g in range(4):
            pcs[39 + cb * 4 + g] = [("w_down", g * 2048, cb * 512, 512, 0)]
    return pcs


def blk_group(b):
    return 0 if b < 19 else (1 if b < 23 else (2 if b < 39 else 3))


def build(SEGM, NL=L, dbg=False):
    nseg = len(SEGM)
    NM = sum(SEGM)
    NT = 4 * NM
    NTOK = NT * 128
    NSD = 2 * nseg
    seg_m0 = [sum(SEGM[:s]) for s in range(nseg)]

    nc = bass.Bass("TRN2", target_bir_lowering=False)

    def din(name, shape, dt=F32):
        return nc.dram_tensor(name, list(shape), dt, kind="ExternalInput").ap()

    def dint(name, shape, dt=F32):
        return nc.dram_tensor(name, list(shape), dt, kind=("ExternalOutput" if (dbg and name in DBG_OUT) else "Internal")).ap()

    x_d = din("x", [NTOK, D])
    wsrc = {"w_in": din("w_in", [NL, D, DIN]), "w_out": din("w_out", [NL, D, D]),
            "w_up": din("w_up", [NL, D, DFF]), "w_down": din("w_down", [NL, DFF, D])}
    w2pad_d = din("w2pad", [32, L * 2 * 512])
    blr_d = din("blr", [128, L * 8])
    lbp_d = din("lbp", [128, 32])
    gain_d = din("gainrep", [L, 128, D])
    ln_d = {k: din(k, [L, 128, D]) for k in ("ln1g", "ln1b", "ln2g", "ln2b")}
    ident_d = din("ident", [128, 128])
    maskf_d = din("maskf", [128, 128])
    maskb_d = din("maskb", [128, 128])
    scanm_d = din("scanmask", [128, 512])
    onehot_d = din("onehot", [128, 8])
    y_d = nc.dram_tensor("y", [NTOK, D], F32, kind="ExternalOutput").ap()

    wbf_d = dint("wbf", [L * NBLK, 128, 8192], BF16)
    hbuf_d = dint("hbuf", [NTOK, D])
    h1buf_d = dint("h1buf", [NTOK, D])
    opf_d = dint("opf", [NT, 128, 24 * 384], BF16)
    vbuf_d = dint("vbuf", [NT, 128, D], BF16)
    gbuf_d = dint("gbuf", [NT, 128, D])
    slf_d = dint("slf", [NT, 128, D])
    slb_d = dint("slb", [NT, 128, D])
    slend_d = dint("slend", [NM, 128, D])
    gin_d = dint("gin", [NSD * 128, 2064])
    gout_d = dint("gout", [8 * NSD * 128, 2064])

    P = Prog()
    Rhbuf = R("hbuf")
    Rh1buf = R("h1buf")
    es = ExitStack()
    ARENA = 53000
    A = es.enter_context(nc.sbuf_tensor("arena", [128, ARENA], F32))
    pb = [es.enter_context(nc.psum_tensor(f"pb{i}", [128, 512], F32)) for i in range(6)]
    pt = [es.enter_context(nc.psum_tensor(f"pt{i}", [128, 8, 128], BF16)) for i in range(2)]
    Rpb = [R(f"pb{i}", True) for i in range(6)]
    Rpt = [R(f"pt{i}", True) for i in range(2)]

    class Arena:
        def __init__(self):
            self.off = 0

        def f(self, n):
            a = A[:, self.off:self.off + n]
            self.off += n
            assert self.off <= ARENA, ("arena overflow", self.off)
            return a

        def b(self, n):
            assert n % 2 == 0
            a = A[:, self.off:self.off + n // 2].bitcast(BF16)
            self.off += n // 2
            assert self.off <= ARENA, ("arena overflow", self.off)
            return a

    ar = Arena()
    identb = ar.b(128); Ridb = R("identb")
    maskfb = ar.b(128); maskbb = ar.b(128); Rmask = R("mask")
    scanm = ar.f(512); Rscanm = R("scanm")
    onehot = ar.f(8); Roh = R("onehot")
    w2b = ar.b(L * 2 * 512); Rw2b = R("w2b")
    nb = ar.f(L * 8); Rnb = R("nb")
    LB = ar.f(32); OML = ar.f(32); NOML = ar.f(32); Rlb = R("lb")
    DwF = ar.f(NT * 12); DcB = ar.f(NT * 12); AmF = ar.f(NM * 12); LmF = ar.f(NM * 12)
    Rtab = R("tables")
    tot = ar.f(96); Rtot = [R(f"tot{i}") for i in range(24)]
    runb = ar.f(12); Rrunb = R("runb")
    sml = ar.f(256); Rsml = R("sml")
    atile = ar.f(8); Ratile = [R("atile0"), R("atile1")]
    PBASE = ar.off

    dkeys = []

    def dk(name):
        dkeys.append(name)
        return name

    def lbidx(l, d, h):
        return (l * 2 + d) * 8 + h

    pcs = block_pieces()
    Rwg = [[R(f"wg{l}_{g}") for g in range(4)] for l in range(L)]
    conv_list = [(l, b) for l in range(NL) for b in range(NBLK)]
    conv_pos = [0]
    conv_done = set()

    def emit_conv(k):
        while k > 0 and conv_pos[0] < len(conv_list):
            (l, b) = conv_list[conv_pos[0]]
            conv_pos[0] += 1
            k -= 1
            g = blk_group(b)
            key = f"wcv{l}_{g}"
            if key not in dkeys:
                dk(key)
            if b == 10:
                dst3 = wbf_d[l * NBLK + b][:, 0:512].rearrange("p (kc c) -> p kc c", c=32)
            else:
                dst3 = wbf_d[l * NBLK + b].rearrange("p (kc c) -> p kc c", c=512)
            for (tn, r0, c0, n, off) in pcs[b]:
                src = wsrc[tn][l, r0:r0 + 2048, c0:c0 + n].rearrange("(kc p) c -> p kc c", p=128)
                o = P.op("pool", lambda e, d_=dst3[:, :, off:off + n], s_=src: e.dma_start(out=d_, in_=s_), dma=key)
                Rwg[l][g].w = o
            conv_done.add((l, b))

    def conv_tick(l, phase):
        if l == 0 and phase == 1:
            emit_conv(-(-36 // NM))
        elif l == 0 and phase in (2, 3) and NL > 1:
            if conv_pos[0] < NBLK:
                emit_conv(NBLK - conv_pos[0])
            emit_conv(-(-NBLK // (2 * NM)))

    emit_conv(19)

    ar.off = PBASE
    tmpf = ar.f(4096); Rtmpf = R("tmpf")
    k_c = dk("const")

    def ld_const(dst, src, rs, eng="sp"):
        P.op(eng, lambda e, d_=dst, s_=src: e.dma_start(out=d_, in_=s_), wr=rs, dma=k_c)

    ld_const(tmpf[:, 0:128], ident_d[:, :], [Rtmpf])
    ld_const(tmpf[:, 128:256], maskf_d[:, :], [Rtmpf])
    ld_const(tmpf[:, 256:384], maskb_d[:, :], [Rtmpf])
    ld_const(scanm, scanm_d[:, :], [Rscanm])
    ld_const(onehot, onehot_d[:, :], [Roh])
    ld_const(tmpf[0:32, 384:384 + 2048], w2pad_d[:, :], [Rtmpf])
    ld_const(nb, blr_d[:, :], [Rnb])
    ld_const(sml[:, 0:32], lbp_d[:, :], [Rsml])
    P.barrier()
    P.op("dve", lambda e: e.tensor_copy(identb, tmpf[:, 0:128]), [Rtmpf], [Ridb])
    P.op("dve", lambda e: e.tensor_copy(maskfb, tmpf[:, 128:256]), [Rtmpf], [Rmask])
    P.op("dve", lambda e: e.tensor_copy(maskbb, tmpf[:, 256:384]), [Rtmpf], [Rmask])
    P.op("dve", lambda e: e.tensor_copy(w2b[0:32, :], tmpf[0:32, 384:384 + 2048]), [Rtmpf], [Rw2b])
    P.op("dve", lambda e: e.tensor_scalar(out=nb, in0=nb, scalar1=-1.0, scalar2=None, op0=ALU.mult), [Rnb], [Rnb])
    P.op("act", lambda e: e.activation(out=sml[:, 32:64], in_=sml[:, 0:32], func=AF.Exp), [Rsml], [Rsml])
    for d in range(2):
        e0 = sml[:, 32 + d * 16: 32 + d * 16 + 8]
        e1 = sml[:, 32 + d * 16 + 8: 32 + d * 16 + 16]
        s_ = sml[:, 64 + d * 8: 64 + d * 8 + 8]
        r_ = sml[:, 80 + d * 8: 80 + d * 8 + 8]
        P.op("dve", lambda e, a=e0, b=e1, o_=s_: e.tensor_tensor(out=o_, in0=a, in1=b, op=ALU.add), [Rsml], [Rsml])
        P.op("dve", lambda e, a=s_, o_=r_: e.reciprocal(o_, a), [Rsml], [Rsml])
        P.op("dve", lambda e, o_=LB[:, lbidx(0, d, 0):lbidx(0, d, 0) + 8]: e.memset(o_, 0.0), [], [Rlb])
        P.op("dve", lambda e, a=e1, b=r_, o_=LB[:, lbidx(1, d, 0):lbidx(1, d, 0) + 8]: e.tensor_tensor(out=o_, in0=a, in1=b, op=ALU.mult),
             [Rsml], [Rlb])
    P.op("dve", lambda e: e.tensor_scalar(out=OML, in0=LB, scalar1=-1.0, scalar2=1.0, op0=ALU.mult, op1=ALU.add), [Rlb], [Rlb])
    P.op("dve", lambda e: e.tensor_scalar(out=NOML, in0=LB, scalar1=1.0, scalar2=-1.0, op0=ALU.mult, op1=ALU.add), [Rlb], [Rlb])
    P.barrier()

    class Ring:
        def __init__(self, slots, name):
            self.slots = slots
            self.rs = [R(f"{name}{i}") for i in range(len(slots))]
            self.keys = [dk(f"{name}{i}") for i in range(len(slots))]
            self.seq = []
            self.em = 0
            self.us = 0

        def extend(self, seq):
            self.seq.extend(seq)

        def _emit(self):
            (l, b) = self.seq[self.em]
            assert (l, b) in conv_done, ("weight block used before conversion emitted", l, b)
            i = self.em % len(self.slots)
            ncol = 384 if 2 <= b < 10 else 512
            src = wbf_d[l * NBLK + b].rearrange("p (kc c) -> p kc c", c=512)[:, :, 0:ncol]
            dst = self.slots[i][:, :, 0:ncol]
            P.op("sp", lambda e, d_=dst, s_=src: e.dma_start(out=d_, in_=s_), [Rwg[l][blk_group(b)]], [self.rs[i]], dma=self.keys[i])
            self.em += 1

        def next(self, expect):
            assert self.seq[self.us] == expect, (self.seq[self.us], expect)
            while self.em < len(self.seq) and self.em < self.us + len(self.slots):
                self._emit()
            i = self.us % len(self.slots)
            self.us += 1
            return self.slots[i], self.rs[i]

    def mmgroup(out, pairs, rd, wr):
        n = len(pairs)

        def fn(e, out=out, pairs=pairs, n=n):
            ins = None
            for i, (a, b) in enumerate(pairs):
                ins = e.matmul(out, a, b, start=(i == 0), stop=(i == n - 1))
            return ins
        P.op("pe", fn, rd, wr)

    def v3(ap, t=128):
        return ap.rearrange("p (j t) -> p j t", t=t)

    def tile_of(s, m, j):
        return (seg_m0[s] + m) * 4 + j

    def hcols(is_hg, h):
        return (1024 + h * 128, 128) if is_hg else (h * 256, 256)

    def hc12(hh):
        return hcols(hh >= 4, hh - 4 if hh >= 4 else hh)

    def layer_pass1(l):
        in_d = x_d if l == 0 else hbuf_d

        ar.off = PBASE
        gain = ar.f(2048); Rgain = R("gain")
        xT = ar.b(16 * 512).rearrange("p (k c) -> p k c", c=512); RxT = [R(f"xT{j}") for j in range(4)]
        xb = [ar.b(2048) for _ in range(2)]; Rxb = [R("xb0"), R("xb1")]; kxb = [dk(f"p1xb_{i}") for i in range(2)]
        ring = Ring([ar.b(16 * 512).rearrange("p (k c) -> p k c", c=512) for _ in range(2)], "p1w_")
        wglr = ar.b(16 * 32).rearrange("p (k c) -> p k c", c=32); Rwglr = R("wglr"); kwglr = dk(f"p1wglr_")
        glrT = ar.b(512); RglrT = R("glrT")
        qs = [ar.f(512) for _ in range(2)]; Rqs = [R("qs0"), R("qs1")]
        ksA = [ar.f(512) for _ in range(2)]; RksA = [R("ksA0"), R("ksA1")]
        ksB = [ar.f(512) for _ in range(2)]; RksB = [R("ksB0"), R("ksB1")]
        gts = [[ar.f(512) for _ in range(3)] for _ in range(2)]; Rgts = [[R(f"gt{p_}_{i}") for i in range(3)] for p_ in range(2)]
        Lb = [ar.f(512) for _ in range(2)]; RLb = [R("Lb0"), R("Lb1")]
        scs = [[ar.f(512) for _ in range(4)] for _ in range(2)]; Rscs = [[R(f"sc{p_}_{i}") for i in range(4)] for p_ in range(2)]
        Es = [[ar.f(512) for _ in range(4)] for _ in range(2)]; REs = [[R(f"E{p_}_{i}") for i in range(4)] for p_ in range(2)]
        stage = [ar.b(4 * 384) for _ in range(2)]; Rstage = [[R(f"stage{p_}_{i}") for i in range(3)] for p_ in range(2)]; kstage = [dk(f"p1st_{i}") for i in range(2)]
        k3f = [ar.b(512) for _ in range(2)]; Rk3f = [R("k3f0"), R("k3f1")]
        k3T = [ar.b(512) for _ in range(2)]; Rk3T = [R("k3T0"), R("k3T1")]
        vt = [ar.b(2048) for _ in range(4)]; Rvt = [R(f"vt{j}") for j in range(4)]; kvt = [dk(f"p1vt_{j}") for j in range(4)]
        silt = [ar.f(512) for _ in range(2)]; Rsilt = [R("silt0"), R("silt1")]
        Gst = [ar.f(512) for _ in range(2)]; RGst = [R("Gst0"), R("Gst1")]; kGst = [dk(f"p1G_{i}") for i in range(2)]
        Sstf = ar.f(1024); Sstb = ar.f(1024); RSstf = R("Sstf"); RSstb = R("Sstb"); kSst = [dk(f"p1Sf_"), dk(f"p1Sb_")]
        Sb = ar.f(2048); RSb = [R(f"Sb{h}") for h in range(12)]
        Send = ar.f(2048); RSend = R("Send"); kSend = dk(f"p1Send_")
        ginS = ar.f(2064); RginS = R("ginS"); kgin = dk(f"gin_")
        k_misc = dk(f"p1misc_")

        P.op("sp", lambda e, g_=gain, s_=gain_d[l]: e.dma_start(out=g_, in_=s_), [], [Rgain], dma=k_misc)

        pbrot = Rot([0, 1, 2, 3])
        tmrot = Rot([4, 5])
        ptrot = Rot([0, 1])
        parity = [0]

        seqm = [(l, b) for b in (11, 12, 13, 14, 0, 1, 2, 3, 4, 5, 6, 7, 8, 9, 15, 16, 17, 18)]
        ring.extend(seqm * NM)

        def scanprep(n, hd, isb, qA, Rq, kA, Rk, gA, Rg, sc, gt0, is_hg, h):
            par = n % 2
            c, d0, d1, d2 = scs[par]
            Rsc = Rscs[par]
            E, RE = Es[par], REs[par]
            c3, d03, d13, d23 = v3(c), v3(d0), v3(d1), v3(d2)
            mid = c3[:, :, 63:64].to_broadcast([128, 4, 128])
            last = c3[:, :, 127:128].to_broadcast([128, 4, 128])
            sp_ = par
            ap_ = par
            at = atile[:, ap_ * 4:ap_ * 4 + 4]
            c0, dv = hcols(is_hg, h)
            hh = (4 + h) if is_hg else h

            def stA():
                P.op("dve", lambda e: e.tensor_tensor_scan(c, scanm, gA, 0.0, ALU.mult, ALU.add), [Rg, Rscanm], [Rsc[0]])
                if not isb:
                    P.op("dve", lambda e: e.tensor_tensor(out=d13, in0=c3, in1=mid, op=ALU.subtract), [Rsc[0]], [Rsc[2]])
                    P.op("dve", lambda e: e.tensor_tensor(out=d23, in0=c3, in1=last, op=ALU.subtract), [Rsc[0]], [Rsc[3]])
                else:
                    P.op("dve", lambda e: e.tensor_tensor(out=d0, in0=c, in1=gA, op=ALU.subtract), [Rsc[0], Rg], [Rsc[1]])
                    P.op("dve", lambda e: e.tensor_tensor(out=d13, in0=d03, in1=last, op=ALU.subtract), [Rsc[0], Rsc[1]], [Rsc[2]])
                    P.op("dve", lambda e: e.tensor_tensor(out=d23, in0=d03, in1=mid, op=ALU.subtract), [Rsc[0], Rsc[1]], [Rsc[3]])
                P.op("dve", lambda e, o_=tot[:, hd * 4:hd * 4 + 4], i_=c3[:, :, 127]: e.tensor_scalar(out=o_, in0=i_, scalar1=float(sc), scalar2=None, op0=ALU.mult),
                     [Rsc[0]], [Rtot[hd]])

            def stB1():
                if not isb:
                    exps = [(c, Rsc[0], sc), (d1, Rsc[2], sc), (d1, Rsc[2], -sc), (d2, Rsc[3], -sc)]
                else:
                    exps = [(d1, Rsc[2], -sc), (d2, Rsc[3], -sc), (d2, Rsc[3], sc), (d0, Rsc[1], sc)]
                for i, (src, rs, s_) in enumerate(exps):
                    P.op("act", lambda e, o_=E[i], i_=src, s_=s_: e.activation(out=o_, in_=i_, func=AF.Exp, scale=float(s_)), [rs], [RE[i]])
                P.op("act", lambda e, o_=at, i_=tot[:, hd * 4:hd * 4 + 4]: e.activation(out=o_, in_=i_, func=AF.Exp), [Rtot[hd]], [Ratile[ap_]])
                stg = stage[sp_].rearrange("p (j o t) -> p j o t", o=3, t=128)
                srcs = [(qA, Rq), (qA, Rq), (kA, Rk)]
                for i in range(3):
                    eng_ = "pool" if i != 1 else "dve"
                    P.op(eng_, lambda e, o_=stg[:, :, i, :], a=v3(srcs[i][0]), b=v3(E[i]): e.tensor_tensor(out=o_, in0=a, in1=b, op=ALU.mult),
                         [srcs[i][1], RE[i]], [Rstage[sp_][i]])
                P.op("dve", lambda e, o_=k3f[sp_], a=kA, b=E[3]: e.tensor_tensor(out=o_, in0=a, in1=b, op=ALU.mult), [Rk, RE[3]], [Rk3f[sp_]])
                dst = opf_d[gt0:gt0 + 4, :, hd * 384:(hd + 1) * 384].rearrange("j p x -> p j x")
                P.op("pool", lambda e, d_=dst, s_=stage[sp_].rearrange("p (j x) -> p j x", x=384): e.dma_start(out=d_, in_=s_),
                     Rstage[sp_], [], dma=kstage[sp_])

            def stB2():
                pi = ptrot.next()
                for j in range(4):
                    P.op("pe", lambda e, o_=pt[pi][:, j, :], i_=k3f[sp_][:, j * 128:(j + 1) * 128]: e.transpose(o_, i_, identb),
                         [Rk3f[sp_], Ridb], [Rpt[pi]])
                P.op("dve", lambda e, o_=v3(k3T[sp_]), i_=pt[pi][:, 0:4, :]: e.tensor_copy(o_, i_), [Rpt[pi]], [Rk3T[sp_]])
                if is_hg:
                    bi = tmrot.next()
                    banks = [bi]
                    outs = [pb[bi][:, j * 128:(j + 1) * 128] for j in range(4)]
                else:
                    banks = [4, 5]
                    outs = [pb[4 + j // 2][:, (j % 2) * 256:(j % 2) * 256 + 256] for j in range(4)]
                for j in range(4):
                    P.op("pe", lambda e, o_=outs[j], a=k3T[sp_][:, j * 128:(j + 1) * 128], b=vt[j][:, c0:c0 + dv]: e.matmul(o_, a, b, start=True, stop=True),
                         [Rk3T[sp_], Rvt[j]], [Rpb[banks[0] if is_hg else 4 + j // 2]])
                Rb = [Rpb[b_] for b_ in banks]
                if not isb:
                    S3 = Sstf[:, 0:4 * dv].rearrange("p (j x) -> p j x", x=dv)
                    P.op("dve", lambda e, o_=S3[:, 0, :]: e.memset(o_, 0.0), [], [RSstf])
                    P.op("dve", lambda e, o_=S3[:, 1, :], i_=outs[0]: e.tensor_copy(o_, i_), Rb, [RSstf])
                    for j in (1, 2):
                        P.op("dve", lambda e, o_=S3[:, j + 1, :], a=S3[:, j, :], s_=at[:, j:j + 1], b=outs[j]:
                             e.scalar_tensor_tensor(out=o_, in0=a, scalar=s_, in1=b, op0=ALU.mult, op1=ALU.add), Rb + [Ratile[ap_], RSstf], [RSstf])
                    P.op("dve", lambda e, o_=Send[:, c0:c0 + dv], a=S3[:, 3, :], s_=at[:, 3:4], b=outs[3]:
                         e.scalar_tensor_tensor(out=o_, in0=a, scalar=s_, in1=b, op0=ALU.mult, op1=ALU.add), Rb + [Ratile[ap_], RSstf], [RSend])
                    dstS = slf_d[gt0:gt0 + 4, :, c0:c0 + dv].rearrange("j p x -> p j x")
                    P.op("pool", lambda e, d_=dstS, s_=S3: e.dma_start(out=d_, in_=s_), [RSstf], [], dma=kSst[0])
                else:
                    S3 = Sstb[:, 0:4 * dv].rearrange("p (j x) -> p j x", x=dv)
                    sbh = Sb[:, c0:c0 + dv]
                    for j in (3, 2, 1, 0):
                        P.op("pool", lambda e, o_=S3[:, j, :], i_=sbh: e.tensor_copy(o_, i_), [RSb[hh]], [RSstb])
                        P.op("dve", lambda e, o_=sbh, s_=at[:, j:j + 1], b=outs[j]:
                             e.scalar_tensor_tensor(out=o_, in0=o_, scalar=s_, in1=b, op0=ALU.mult, op1=ALU.add), Rb + [Ratile[ap_], RSb[hh]], [RSb[hh]])
                    dstS = slb_d[gt0:gt0 + 4, :, c0:c0 + dv].rearrange("j p x -> p j x")
                    P.op("pool", lambda e, d_=dstS, s_=S3: e.dma_start(out=d_, in_=s_), [RSstb], [], dma=kSst[1])

            return stA, stB1, stB2

        totf = tot[:, 0:48].rearrange("p (h j) -> p h j", j=4)
        totb = tot[:, 48:96].rearrange("p (h j) -> p h j", j=4)
        pf = sml[:, 128:176].rearrange("p (h j) -> p h j", j=4)
        sf = sml[:, 176:224].rearrange("p (h j) -> p h j", j=4)

        def tables(gm, gt0):
            P.op("dve", lambda e: e.memset(pf[:, :, 0], 0.0), Rtot, [Rsml])
            for j in (1, 2, 3):
                P.op("dve", lambda e, j=j: e.tensor_tensor(out=pf[:, :, j], in0=pf[:, :, j - 1], in1=totf[:, :, j - 1], op=ALU.add), Rtot + [Rsml], [Rsml])
            dwv = DwF[:, gt0 * 12:(gt0 + 4) * 12].rearrange("p (j h) -> p h j", h=12)
            P.op("act", lambda e: e.activation(out=dwv, in_=pf, func=AF.Exp), [Rsml], [Rtab])
            lm = LmF[:, gm * 12:(gm + 1) * 12]
            am = AmF[:, gm * 12:(gm + 1) * 12]
            P.op("dve", lambda e: e.tensor_tensor(out=lm, in0=pf[:, :, 3], in1=totf[:, :, 3], op=ALU.add), Rtot + [Rsml], [Rtab])
            P.op("act", lambda e: e.activation(out=am, in_=lm, func=AF.Exp), [Rtab], [Rtab])
            P.op("dve", lambda e: e.tensor_copy(sf[:, :, 3], runb), [Rrunb, Rsml], [Rsml])
            for j in (2, 1, 0):
                P.op("dve", lambda e, j=j: e.tensor_tensor(out=sf[:, :, j], in0=sf[:, :, j + 1], in1=totb[:, :, j + 1], op=ALU.add), Rtot + [Rsml], [Rsml])
            dcv = DcB[:, gt0 * 12:(gt0 + 4) * 12].rearrange("p (j h) -> p h j", h=12)
            P.op("act", lambda e: e.activation(out=dcv, in_=sf, func=AF.Exp), [Rsml], [Rtab])
            P.op("dve", lambda e: e.tensor_tensor(out=runb, in0=sf[:, :, 0], in1=totb[:, :, 0], op=ALU.add), Rtot + [Rsml], [Rrunb])

        for s in range(nseg):
            P.op("pool", lambda e: e.memset(Sb, 0.0), [], RSb)
            P.op("dve", lambda e: e.memset(runb, 0.0), [], [Rrunb])
            for m in reversed(range(SEGM[s])):
                gm = seg_m0[s] + m
                gt0 = gm * 4
                for j in range(4):
                    xi = j % 2
                    P.op("pool", lambda e, d_=xb[xi], s_=in_d[(gt0 + j) * 128:(gt0 + j + 1) * 128, :]: e.dma_start(out=d_, in_=s_),
                         [Rhbuf] if l > 0 else [], [Rxb[xi]], dma=kxb[xi])
                    for half in range(2):
                        pi = half
                        for k in range(8):
                            kc = half * 8 + k
                            P.op("pe", lambda e, o_=pt[pi][:, k, :], i_=xb[xi][:, kc * 128:(kc + 1) * 128]: e.transpose(o_, i_, identb),
                                 [Rxb[xi], Ridb], [Rpt[pi]])
                        if half == 0:
                            P.op("act", lambda e, o_=xT[:, 0:8, j * 128:(j + 1) * 128], i_=pt[pi][:, :, :]: e.activation(out=o_, in_=i_, func=AF.Copy),
                                 [Rpt[pi]], [RxT[j]])
                        else:
                            P.op("dve", lambda e, o_=xT[:, 8:16, j * 128:(j + 1) * 128], i_=pt[pi][:, :, :]: e.tensor_copy(o_, i_),
                                 [Rpt[pi]], [RxT[j]])
                P.op("sp", lambda e, d_=wglr, s_=wbf_d[l * NBLK + 10][:, 0:512].rearrange("p (kc c) -> p kc c", c=32): e.dma_start(out=d_, in_=s_),
                     [Rwg[l][0]], [Rwglr], dma=kwglr)
                for vb in range(4):
                    W, RW = ring.next((l, 11 + vb))
                    for j in range(4):
                        bi = tmrot.next()
                        mmgroup(pb[bi][:, :], [(xT[:, kc, j * 128:(j + 1) * 128], W[:, kc, :]) for kc in range(16)], [RxT[j], RW], [Rpb[bi]])
                        P.op("dve", lambda e, o_=vt[j][:, vb * 512:(vb + 1) * 512], i_=pb[bi][:, :]: e.tensor_copy(o_, i_), [Rpb[bi]], [Rvt[j]])
                for j in range(4):
                    P.op("pool", lambda e, d_=vbuf_d[gt0 + j], s_=vt[j]: e.dma_start(out=d_, in_=s_), [Rvt[j]], [], dma=kvt[j])
                bi = pbrot.next()
                mmgroup(pb[bi][0:32, :], [(wglr[:, kc, :], xT[:, kc, :]) for kc in range(16)], RxT + [Rwglr], [Rpb[bi]])
                P.op("act", lambda e, i_=pb[bi][0:32, :]: e.activation(out=glrT[0:32, :], in_=i_, func=AF.Copy), [Rpb[bi]], [RglrT])
                tasks = []
                wslot = {}

                def gla_s1(n, h, d):
                    hp, i = h // 2, h % 2
                    qi = h % 2
                    par = n % 2
                    gt1 = gts[par][0]
                    if d == 0:
                        if i == 0:
                            wslot[hp] = ring.next((l, hp))
                        W, RW = wslot[hp]
                        bq = pbrot.next()
                        mmgroup(pb[bq][:, :], [(W[:, kc, i * 256:i * 256 + 128], xT[:, kc, :]) for kc in range(16)], RxT + [RW], [Rpb[bq]])
                        P.op("act", lambda e, o_=qs[qi], i_=pb[bq][:, :]: e.activation(out=o_, in_=i_, func=AF.Copy, scale=float(DK ** -0.5)), [Rpb[bq]], [Rqs[qi]])
                        bk = pbrot.next()
                        mmgroup(pb[bk][:, :], [(W[:, kc, i * 256 + 128:i * 256 + 256], xT[:, kc, :]) for kc in range(16)], RxT + [RW], [Rpb[bk]])
                        P.op("act", lambda e, o_=ksA[qi], i_=pb[bk][:, :]: e.activation(out=o_, in_=i_, func=AF.Copy), [Rpb[bk]], [RksA[qi]])
                    bg = pbrot.next()
                    wof = (l * 2 + d) * 512 + h * 128
                    P.op("pe", lambda e, o_=pb[bg][:, :], a=w2b[0:32, wof:wof + 128], b=glrT[0:32, :]: e.matmul(o_, a, b, start=True, stop=True),
                         [Rw2b, RglrT], [Rpb[bg]])
                    nbi = (l * 2 + d) * 4 + h
                    P.op("act", lambda e, i_=pb[bg][:, :], b_=nb[:, nbi:nbi + 1]: e.activation(out=gt1, in_=i_, func=AF.Exp, scale=-1.0, bias=b_),
                         [Rpb[bg], Rnb], [Rgts[par][0]])
                    P.op("act", lambda e, o_=Lb[par]: e.activation(out=o_, in_=gt1, func=AF.Ln, bias=1.0), [Rgts[par][0]], [RLb[par]])

                def hg_s1(n, h, d):
                    qi = h % 2
                    par = n % 2
                    gt1, gt2, gt3 = gts[par]
                    Rg1, Rg2, Rg3 = Rgts[par]
                    if d == 0:
                        wslot[2 + h] = ring.next((l, 2 + h))
                        W, RW = wslot[2 + h]
                        bq = pbrot.next()
                        mmgroup(pb[bq][:, :], [(W[:, kc, 0:128], xT[:, kc, :]) for kc in range(16)], RxT + [RW], [Rpb[bq]])
                        P.op("act", lambda e, o_=qs[qi], i_=pb[bq][:, :]: e.activation(out=o_, in_=i_, func=AF.Copy), [Rpb[bq]], [Rqs[qi]])
                    W, RW = wslot[2 + h]
                    bz = pbrot.next()
                    mmgroup(pb[bz][:, :], [(W[:, kc, 128 + d * 128:256 + d * 128], xT[:, kc, :]) for kc in range(16)], RxT + [RW], [Rpb[bz]])
                    P.op("act", lambda e, i_=pb[bz][:, :]: e.activation(out=gt1, in_=i_, func=AF.Exp, scale=-1.0), [Rpb[bz]], [Rg1])
                    P.op("act", lambda e: e.activation(out=gt2, in_=gt1, func=AF.Ln, bias=1.0), [Rg1], [Rg2])
                    P.op("act", lambda e: e.activation(out=gt3, in_=gt2, func=AF.Exp, scale=-1.0), [Rg2], [Rg3])
                    ix = lbidx(l, d, h)
                    P.op("act", lambda e, o_=Lb[par], s_=OML[:, ix:ix + 1], b_=LB[:, ix:ix + 1]: e.activation(out=o_, in_=gt3, func=AF.Ln, scale=s_, bias=b_),
                         [Rg3, Rlb], [RLb[par]])
                    kbuf, Rkbuf = (ksA[qi], RksA[qi]) if d == 0 else (ksB[qi], RksB[qi])
                    P.op("pool", lambda e, o_=kbuf, s1=NOML[:, ix:ix + 1], s2=OML[:, ix:ix + 1]:
                         e.tensor_scalar(out=o_, in0=gt3, scalar1=s1, scalar2=s2, op0=ALU.mult, op1=ALU.add), [Rg3, Rlb], [Rkbuf])

                n_ = 0
                for h in range(4):
                    for d in range(2):
                        qi = h % 2
                        stA, stB1, stB2 = scanprep(n_, d * 12 + h, d == 1, qs[qi], Rqs[qi], ksA[qi], RksA[qi], Lb[n_ % 2], RLb[n_ % 2], -1.0 / 16.0, gt0, False, h)
                        tasks.append((lambda n=n_, h=h, d=d: gla_s1(n, h, d), stA, stB1, stB2))
                        n_ += 1
                for h in range(8):
                    for d in range(2):
                        qi = h % 2
                        kbuf, Rkbuf = (ksA[qi], RksA[qi]) if d == 0 else (ksB[qi], RksB[qi])
                        stA, stB1, stB2 = scanprep(n_, d * 12 + 4 + h, d == 1, qs[qi], Rqs[qi], kbuf, Rkbuf, Lb[n_ % 2], RLb[n_ % 2], 1.0, gt0, True, h)
                        tasks.append((lambda n=n_, h=h, d=d: hg_s1(n, h, d), stA, stB1, stB2))
                        n_ += 1
                NTk = len(tasks)
                for it in range(NTk + 2):
                    if it < NTk:
                        tasks[it][0]()
                        tasks[it][1]()
                    if 0 <= it - 1 < NTk:
                        tasks[it - 1][2]()
                    if 0 <= it - 2 < NTk:
                        tasks[it - 2][3]()
                P.op("pool", lambda e, d_=slend_d[gm], s_=Send: e.dma_start(out=d_, in_=s_), [RSend], [], dma=kSend)
                for gb in range(4):
                    W, RW = ring.next((l, 15 + gb))
                    for j in range(4):
                        bi = tmrot.next()
                        mmgroup(pb[bi][:, :], [(xT[:, kc, j * 128:(j + 1) * 128], W[:, kc, :]) for kc in range(16)], [RxT[j], RW], [Rpb[bi]])
                        gi = (gb * 4 + j) % 2
                        P.op("act", lambda e, o_=silt[gi], i_=pb[bi][:, :]: e.activation(out=o_, in_=i_, func=AF.Silu), [Rpb[bi]], [Rsilt[gi]])
                        P.op("pool", lambda e, o_=Gst[gi], a=silt[gi], b=gain[:, gb * 512:(gb + 1) * 512]: e.tensor_tensor(out=o_, in0=a, in1=b, op=ALU.mult),
                             [Rsilt[gi], Rgain], [RGst[gi]])
                        P.op("pool", lambda e, d_=gbuf_d[gt0 + j][:, gb * 512:(gb + 1) * 512], s_=Gst[gi]: e.dma_start(out=d_, in_=s_), [RGst[gi]], [], dma=kGst[gi])
                tables(gm, gt0)
                conv_tick(l, 1)
            P.op("dve", lambda e: e.tensor_copy(ginS[:, 0:2048], Sb), RSb, [RginS])
            P.op("act", lambda e: e.activation(out=ginS[:, 2048:2060], in_=runb, func=AF.Exp), [Rrunb], [RginS])
            P.op("dve", lambda e: e.memset(ginS[:, 2060:2064], 0.0), [], [RginS])
            P.op("pool", lambda e, d_=gin_d[(s * 2 + 1) * 128:(s * 2 + 2) * 128, :]: e.dma_start(out=d_, in_=ginS), [RginS], [], dma=kgin)
        P.barrier()

    def layer_exchange(l):
        ar.off = PBASE
        Tst = ar.f(2048); RT = R("Tst")
        slb_ = [ar.f(2048) for _ in range(2)]; Rslb_ = [R("xsl0"), R("xsl1")]; kx = [dk(f"xsl_{i}") for i in range(2)]
        ginS = ar.f(2064); RginS = R("ginS2")
        slog = ar.f(12); Rslog = R("slog")
        kgin = dk("gin_")
        cnt = 0
        for s in range(nseg):
            P.op("dve", lambda e: e.memset(Tst, 0.0), [], [RT])
            P.op("dve", lambda e: e.memset(slog, 0.0), [], [Rslog])
            for m in range(SEGM[s]):
                gm = seg_m0[s] + m
                bi = cnt % 2
                cnt += 1
                P.op("sp", lambda e, d_=slb_[bi], s_=slend_d[gm]: e.dma_start(out=d_, in_=s_), [], [Rslb_[bi]], dma=kx[bi])
                for hh in range(12):
                    c0, dv = hcols(hh >= 4, hh - 4 if hh >= 4 else hh)
                    P.op("dve", lambda e, o_=Tst[:, c0:c0 + dv], s_=AmF[:, gm * 12 + hh:gm * 12 + hh + 1], b=slb_[bi][:, c0:c0 + dv]:
                         e.scalar_tensor_tensor(out=o_, in0=o_, scalar=s_, in1=b, op0=ALU.mult, op1=ALU.add), [Rslb_[bi], Rtab, RT], [RT])
                P.op("dve", lambda e, gm=gm: e.tensor_tensor(out=slog, in0=slog, in1=LmF[:, gm * 12:(gm + 1) * 12], op=ALU.add), [Rtab, Rslog], [Rslog])
            P.op("dve", lambda e: e.tensor_copy(ginS[:, 0:2048], Tst), [RT], [RginS])
            P.op("act", lambda e: e.activation(out=ginS[:, 2048:2060], in_=slog, func=AF.Exp), [Rslog], [RginS])
            P.op("dve", lambda e: e.memset(ginS[:, 2060:2064], 0.0), [], [RginS])
            P.op("pool", lambda e, d_=gin_d[(s * 2) * 128:(s * 2 + 1) * 128, :]: e.dma_start(out=d_, in_=ginS), [RginS], [], dma=kgin)
        P.barrier()
        kcc = dk(f"cc_")
        P.op("pool", lambda e: e.collective_compute("AllGather", ALU.bypass, replica_groups=[list(range(8))], ins=[gin_d[:, :]], outs=[gout_d[:, :]]),
             [], [], dma=kcc, inc=1)
        P.barrier()

    def layer_pass2(l):
        in_d = x_d if l == 0 else hbuf_d
        ar.off = PBASE
        lnA = ar.f(2048); lnB = ar.f(2048); Rln = R("ln"); kln = dk(f"p2ln_")
        opb = ar.b(24 * 384); Ropb = R("opb"); kopb = dk(f"p2op_")
        v2 = [ar.b(2048) for _ in range(2)]; Rv2 = [R("v20"), R("v21")]; kv2 = [dk(f"p2v_{i}") for i in range(2)]
        G2 = ar.f(2048); RG2 = R("G2"); kG2 = dk(f"p2G_")
        SLf = ar.f(2048); SLb = ar.f(2048); RSLf = R("SLf"); RSLb = R("SLb"); kSL = [dk(f"p2SLf_"), dk(f"p2SLb_")]
        Sfm = ar.f(2048); Sinb = ar.f(2048); RSfm = R("Sfm"); RSinb = R("Sinb")
        Sfb = ar.b(2048); Sbb = ar.b(2048); RSfb = [R(f"Sfb{h}") for h in range(12)]; RSbb = [R(f"Sbb{h}") for h in range(12)]
        scTf = [ar.b(128) for _ in range(8)]; scTb = [ar.b(128) for _ in range(8)]
        RscTf = [R(f"scTf{i}") for i in range(8)]; RscTb = [R(f"scTb{i}") for i in range(8)]
        sgc = [0]
        sq = ar.f(2048); Rsq = R("sq")
        of = [ar.b(2048) for _ in range(2)]; Rof = [R("of0"), R("of1")]
        oT = ar.b(16 * 256).rearrange("p (k c) -> p k c", c=256); RoT = [R("oT0"), R("oT1")]
        ring = Ring([ar.b(16 * 512).rearrange("p (k c) -> p k c", c=512) for _ in range(2)], "p2w_")
        xp = [ar.f(512) for _ in range(4)]; Rxp = [R(f"xp{i}") for i in range(4)]; kxp = [dk(f"p2xp_{i}") for i in range(4)]
        y1 = [ar.f(2048) for _ in range(2)]; Ry1 = [R("y10"), R("y11")]; ky1 = [dk(f"p2y_{i}") for i in range(2)]
        gath = ar.f(2064); Rgath = R("gath"); kgath = dk(f"p2ga_")
        Tst = sq; RT = Rsq
        st = ar.f(128); Rst = R("st")
        P.op("sp", lambda e, s_=ln_d["ln1g"][l]: e.dma_start(out=lnA, in_=s_), [], [Rln], dma=kln)
        P.op("sp", lambda e, s_=ln_d["ln1b"][l]: e.dma_start(out=lnB, in_=s_), [], [Rln], dma=kln)
        for i in range(8):
            P.op("dve", lambda e, o_=scTf[i]: e.memset(o_, 0.0), [], [RscTf[i]])
            P.op("dve", lambda e, o_=scTb[i]: e.memset(o_, 0.0), [], [RscTb[i]])
        ring.extend([(l, 19 + cb) for cb in range(4)] * (2 * NM))
        xprot = Rot([0, 1, 2, 3])

        for s in range(nseg):
            for d, (acc, Racc) in enumerate(((Sfm, RSfm), (Sinb, RSinb))):
                P.op("dve", lambda e: e.memset(Tst, 0.0), [], [RT])
                P.op("dve", lambda e, a=acc: e.memset(a, 0.0), [], [Racc])
                order = list(range(8)) if d == 0 else list(range(7, -1, -1))
                for ri, r in enumerate(order):
                    P.op("dve", lambda e, a=acc, s_=onehot[:, r:r + 1]: e.scalar_tensor_tensor(out=a, in0=Tst, scalar=s_, in1=a, op0=ALU.mult, op1=ALU.add),
                         [RT, Racc, Roh], [Racc])
                    if ri == 7:
                        break
                    row = (r * NSD + s * 2 + d) * 128
                    P.op("sp", lambda e, s_=gout_d[row:row + 128, :]: e.dma_start(out=gath, in_=s_), [], [Rgath], dma=kgath)
                    for hh in range(12):
                        c0, dv = hc12(hh)
                        P.op("dve", lambda e, o_=Tst[:, c0:c0 + dv], s_=gath[:, 2048 + hh:2049 + hh], b=gath[:, c0:c0 + dv]:
                             e.scalar_tensor_tensor(out=o_, in0=o_, scalar=s_, in1=b, op0=ALU.mult, op1=ALU.add), [Rgath, RT], [RT])
            for m in range(SEGM[s]):
                gm = seg_m0[s] + m
                for j in range(4):
                    gt = gm * 4 + j
                    jj = j % 2
                    vi = j % 2
                    P.op("sp", lambda e, s_=opf_d[gt]: e.dma_start(out=opb, in_=s_), [], [Ropb], dma=kopb)
                    P.op("sp", lambda e, d_=v2[vi], s_=vbuf_d[gt]: e.dma_start(out=d_, in_=s_), [], [Rv2[vi]], dma=kv2[vi])
                    P.op("sp", lambda e, s_=gbuf_d[gt]: e.dma_start(out=G2, in_=s_), [], [RG2], dma=kG2)
                    P.op("sp", lambda e, s_=slf_d[gt]: e.dma_start(out=SLf, in_=s_), [], [RSLf], dma=kSL[0])
                    P.op("sp", lambda e, s_=slb_d[gt]: e.dma_start(out=SLb, in_=s_), [], [RSLb], dma=kSL[1])
                    for hh in range(12):
                        c0, dv = hc12(hh)
                        P.op("dve", lambda e, o_=Sfb[:, c0:c0 + dv], a=Sfm[:, c0:c0 + dv], s_=DwF[:, gt * 12 + hh:gt * 12 + hh + 1], b=SLf[:, c0:c0 + dv]:
                             e.scalar_tensor_tensor(out=o_, in0=a, scalar=s_, in1=b, op0=ALU.mult, op1=ALU.add), [RSfm, Rtab, RSLf], [RSfb[hh]])
                        P.op("dve", lambda e, o_=Sbb[:, c0:c0 + dv], a=Sinb[:, c0:c0 + dv], s_=DcB[:, gt * 12 + hh:gt * 12 + hh + 1], b=SLb[:, c0:c0 + dv]:
                             e.scalar_tensor_tensor(out=o_, in0=a, scalar=s_, in1=b, op0=ALU.mult, op1=ALU.add), [RSinb, Rtab, RSLb], [RSbb[hh]])
                    for gi, grp in enumerate(([0, 1, 2, 3], [4, 5, 6, 7], [8, 9, 10, 11])):
                        for d in range(2):
                            for qi, hh in enumerate(grp):
                                hd = d * 12 + hh
                                ps = pb[d][:, qi * 128:(qi + 1) * 128]
                                km = opb[:, hd * 384 + 256:hd * 384 + 384]
                                qm = opb[:, hd * 384 + 128:hd * 384 + 256]
                                P.op("pe", lambda e, o_=ps, a=km, b=qm: e.matmul(o_, a, b, start=True, stop=True), [Ropb], [Rpb[d]])
                        scts = {}
                        for d in range(2):
                            for qi, hh in enumerate(grp):
                                ps = pb[d][:, qi * 128:(qi + 1) * 128]
                                si = (sgc[0] % 2) * 4 + qi
                                if d == 0:
                                    t_, Rt_ = scTf[si], RscTf[si]
                                    P.op("dve", lambda e, o_=t_[:, 64:128], a=ps[:, 64:128], b=maskfb[:, 64:128]: e.tensor_tensor(out=o_, in0=a, in1=b, op=ALU.mult),
                                         [Rpb[d], Rmask], [Rt_])
                                    P.op("dve", lambda e, o_=t_[0:64, 0:64], a=ps[0:64, 0:64], b=maskfb[0:64, 0:64]: e.tensor_tensor(out=o_, in0=a, in1=b, op=ALU.mult),
                                         [Rpb[d], Rmask], [Rt_])
                                else:
                                    t_, Rt_ = scTb[si], RscTb[si]
                                    P.op("dve", lambda e, o_=t_[:, 0:64], a=ps[:, 0:64], b=maskbb[:, 0:64]: e.tensor_tensor(out=o_, in0=a, in1=b, op=ALU.mult),
                                         [Rpb[d], Rmask], [Rt_])
                                    P.op("dve", lambda e, o_=t_[64:128, 64:128], a=ps[64:128, 64:128], b=maskbb[64:128, 64:128]: e.tensor_tensor(out=o_, in0=a, in1=b, op=ALU.mult),
                                         [Rpb[d], Rmask], [Rt_])
                                scts[(d, hh)] = (t_, Rt_)
                        sgc[0] += 1
                        for hh in grp:
                            c0, dv = hc12(hh)
                            if hh < 4:
                                ob, oc = 2 + hh // 2, (hh % 2) * 256
                            else:
                                ob, oc = 4 + (hh - 4) // 4, ((hh - 4) % 4) * 128
                            q1f = opb[:, hh * 384:hh * 384 + 128]
                            q1b = opb[:, (12 + hh) * 384:(12 + hh) * 384 + 128]
                            vh = v2[vi][:, c0:c0 + dv]
                            mmgroup(pb[ob][:, oc:oc + dv], [(scts[(0, hh)][0], vh), (scts[(1, hh)][0], vh), (q1f, Sfb[:, c0:c0 + dv]), (q1b, Sbb[:, c0:c0 + dv])],
                                    [scts[(0, hh)][1], scts[(1, hh)][1], Rv2[vi], Ropb, RSfb[hh], RSbb[hh]], [Rpb[ob]])
                    for q4 in range(4):
                        P.op("act", lambda e, o_=sq[:, q4 * 512:(q4 + 1) * 512], i_=pb[2 + q4][:, :]: e.activation(out=o_, in_=i_, func=AF.Square), [Rpb[2 + q4]], [Rsq])
                    P.op("dve", lambda e: e.tensor_reduce(out=st[:, 0:4], in_=sq[:, 0:1024].rearrange("p (h x) -> p h x", x=256), axis=AX.X, op=ALU.add), [Rsq], [Rst])
                    P.op("dve", lambda e: e.tensor_reduce(out=st[:, 4:12], in_=sq[:, 1024:2048].rearrange("p (h x) -> p h x", x=128), axis=AX.X, op=ALU.add), [Rsq], [Rst])
                    P.op("dve", lambda e: e.tensor_scalar(out=st[:, 0:4], in0=st[:, 0:4], scalar1=1.0 / 256.0, scalar2=RMS_EPS, op0=ALU.mult, op1=ALU.add), [Rst], [Rst])
                    P.op("dve", lambda e: e.tensor_scalar(out=st[:, 4:12], in0=st[:, 4:12], scalar1=1.0 / 128.0, scalar2=RMS_EPS, op0=ALU.mult, op1=ALU.add), [Rst], [Rst])
                    P.op("act", lambda e: e.activation(out=st[:, 16:28], in_=st[:, 0:12], func=AF.Ln), [Rst], [Rst])
                    P.op("act", lambda e: e.activation(out=st[:, 32:44], in_=st[:, 16:28], func=AF.Exp, scale=-0.5), [Rst], [Rst])
                    for hh in range(12):
                        c0, dv = hc12(hh)
                        if hh < 4:
                            ob, oc = 2 + hh // 2, (hh % 2) * 256
                        else:
                            ob, oc = 4 + (hh - 4) // 4, ((hh - 4) % 4) * 128
                        P.op("dve", lambda e, o_=of[jj][:, c0:c0 + dv], a=pb[ob][:, oc:oc + dv], s_=st[:, 32 + hh:33 + hh], b=G2[:, c0:c0 + dv]:
                             e.scalar_tensor_tensor(out=o_, in0=a, scalar=s_, in1=b, op0=ALU.mult, op1=ALU.mult), [Rpb[ob], Rst, RG2], [Rof[jj]])
                    for half in range(2):
                        for k in range(8):
                            kc = half * 8 + k
                            P.op("pe", lambda e, o_=pt[half][:, k, :], i_=of[jj][:, kc * 128:(kc + 1) * 128]: e.transpose(o_, i_, identb),
                                 [Rof[jj], Ridb], [Rpt[half]])
                        P.op("act", lambda e, o_=oT[:, half * 8:half * 8 + 8, jj * 128:(jj + 1) * 128], i_=pt[half][:, :, :]: e.activation(out=o_, in_=i_, func=AF.Copy),
                             [Rpt[half]], [RoT[jj]])
                    if jj == 1:
                        for cb in range(4):
                            W, RW = ring.next((l, 19 + cb))
                            for t2 in range(2):
                                gt2 = gm * 4 + (j - 1) + t2
                                xi = xprot.next()
                                P.op("sp", lambda e, d_=xp[xi], s_=in_d[gt2 * 128:(gt2 + 1) * 128, cb * 512:(cb + 1) * 512]: e.dma_start(out=d_, in_=s_),
                                     [Rhbuf] if l > 0 else [], [Rxp[xi]], dma=kxp[xi])
                                bi = t2
                                mmgroup(pb[bi][:, :], [(oT[:, kc, t2 * 128:(t2 + 1) * 128], W[:, kc, :]) for kc in range(16)], [RoT[t2], RW],
                                        [Rpb[bi]])
                                P.op("dve", lambda e, o_=y1[t2][:, cb * 512:(cb + 1) * 512], a=xp[xi], b=pb[bi][:, :]:
                                     e.scalar_tensor_tensor(out=o_, in0=a, scalar=float(ALPHA), in1=b, op0=ALU.mult, op1=ALU.add), [Rxp[xi], Rpb[bi]], [Ry1[t2]])
                        for t2 in range(2):
                            gt2 = gm * 4 + (j - 1) + t2
                            layernorm(P, y1[t2], Ry1[t2], st, Rst, lnA, lnB, Rln, 64)
                            P.op("pool", lambda e, d_=h1buf_d[gt2 * 128:(gt2 + 1) * 128, :], s_=y1[t2]: e.dma_start(out=d_, in_=s_), [Ry1[t2]], [Rh1buf], dma=ky1[t2])
                conv_tick(l, 2)
                P.op("sp", lambda e, s_=slend_d[gm]: e.dma_start(out=SLf, in_=s_), [], [RSLf], dma=kSL[0])
                for hh in range(12):
                    c0, dv = hc12(hh)
                    P.op("dve", lambda e, o_=Sfm[:, c0:c0 + dv], s_=AmF[:, gm * 12 + hh:gm * 12 + hh + 1], b=SLf[:, c0:c0 + dv]:
                         e.scalar_tensor_tensor(out=o_, in0=o_, scalar=s_, in1=b, op0=ALU.mult, op1=ALU.add), [RSfm, Rtab, RSLf], [RSfm])
        P.barrier()

    def layer_pass3(l):
        out_d = y_d if l == NL - 1 else hbuf_d
        ar.off = PBASE
        lnA = ar.f(2048); lnB = ar.f(2048); Rln = R("ln3"); kln = dk(f"p3ln_")
        h1s = [ar.f(2048) for _ in range(4)]; Rh1s = [R(f"h1s{j}") for j in range(4)]; kh1 = [dk(f"p3h_{j}") for j in range(4)]; ko = [dk(f"p3o_{j}") for j in range(4)]
        hb = ar.b(2048); Rhb = R("hb")
        hT = ar.b(16 * 512).rearrange("p (k c) -> p k c", c=512); RhT = [R(f"hT{j}") for j in range(4)]
        uT = ar.b(64 * 512).rearrange("p (k c) -> p k c", c=512); RuT = [R(f"uT{g}") for g in range(4)]
        ring = Ring([ar.b(16 * 512).rearrange("p (k c) -> p k c", c=512) for _ in range(3)], "p3w_")
        rt = [ar.f(512) for _ in range(2)]; Rrt = [R("rt0"), R("rt1")]
        st = ar.f(64); Rst = R("st3")
        P.op("sp", lambda e, s_=ln_d["ln2g"][l]: e.dma_start(out=lnA, in_=s_), [], [Rln], dma=kln)
        P.op("sp", lambda e, s_=ln_d["ln2b"][l]: e.dma_start(out=lnB, in_=s_), [], [Rln], dma=kln)
        ring.extend(([(l, 23 + ub) for ub in range(16)] + [(l, 39 + cb * 4 + g) for cb in range(4) for g in range(4)]) * NM)
        uprot = Rot([0, 1])
        for gm in range(NM):
            conv_tick(l, 3)
            for j in range(4):
                gt = gm * 4 + j
                P.op("sp", lambda e, d_=h1s[j], s_=h1buf_d[gt * 128:(gt + 1) * 128, :]: e.dma_start(out=d_, in_=s_), [Rh1buf], [Rh1s[j]], dma=kh1[j])
                P.op("pool", lambda e, i_=h1s[j]: e.tensor_copy(hb, i_), [Rh1s[j]], [Rhb])
                for half in range(2):
                    for k in range(8):
                        kc = half * 8 + k
                        P.op("pe", lambda e, o_=pt[half][:, k, :], i_=hb[:, kc * 128:(kc + 1) * 128]: e.transpose(o_, i_, identb),
                             [Rhb, Ridb], [Rpt[half]])
                    P.op("act", lambda e, o_=hT[:, half * 8:half * 8 + 8, j * 128:(j + 1) * 128], i_=pt[half][:, :, :]: e.activation(out=o_, in_=i_, func=AF.Copy),
                         [Rpt[half]], [RhT[j]])
            for ub in range(16):
                W, RW = ring.next((l, 23 + ub))
                for q in range(4):
                    fc = ub * 4 + q
                    bi = uprot.next()
                    mmgroup(pb[bi][:, :], [(W[:, kc, q * 128:(q + 1) * 128], hT[:, kc, :]) for kc in range(16)], RhT + [RW], [Rpb[bi]])
                    ri = fc % 2
                    P.op("act", lambda e, o_=rt[ri], i_=pb[bi][:, :]: e.activation(out=o_, in_=i_, func=AF.Relu), [Rpb[bi]], [Rrt[ri]])
                    P.op("pool", lambda e, o_=uT[:, fc, :], a=rt[ri]: e.tensor_tensor(out=o_, in0=a, in1=a, op=ALU.mult), [Rrt[ri]], [RuT[fc // 16]])
            for cb in range(4):
                for g in range(4):
                    W, RW = ring.next((l, 39 + cb * 4 + g))
                    for j in range(4):
                        def fn(e, j=j, g=g, W=W):
                            ins = None
                            for q in range(16):
                                ins = e.matmul(pb[2 + j][:, :], uT[:, g * 16 + q, j * 128:(j + 1) * 128], W[:, q, :],
                                               start=(g == 0 and q == 0), stop=(g == 3 and q == 15))
                            return ins
                        P.op("pe", fn, [RuT[g], RW], [Rpb[2 + j]])
                for j in range(4):
                    P.op("dve", lambda e, o_=h1s[j][:, cb * 512:(cb + 1) * 512], b=pb[2 + j][:, :]:
                         e.scalar_tensor_tensor(out=o_, in0=o_, scalar=float(ALPHA), in1=b, op0=ALU.mult, op1=ALU.add), [Rpb[2 + j], Rh1s[j]], [Rh1s[j]])
            for j in range(4):
                gt = gm * 4 + j
                layernorm(P, h1s[j], Rh1s[j], st, Rst, lnA, lnB, Rln, 0)
                P.op("pool", lambda e, d_=out_d[gt * 128:(gt + 1) * 128, :], s_=h1s[j]: e.dma_start(out=d_, in_=s_), [Rh1s[j]], [Rhbuf], dma=ko[j])
        P.barrier()

    for l in range(NL):
        layer_pass1(l)
        layer_exchange(l)
        layer_pass2(l)
        layer_pass3(l)

    P.finalize()
    esem = {k: es.enter_context(nc.semaphore(f"e_{k}")) for k in Prog.ENG}
    dsem = {k: es.enter_context(nc.semaphore(f"d_{i}")) for i, k in enumerate(dict.fromkeys(dkeys))}
    with nc.Block() as block:
        @block.tensor
        def _(t):
            P.emit("pe", t, esem, dsem)

        @block.scalar
        def _(a):
            P.emit("act", a, esem, dsem)

        @block.vector
        def _(v):
            P.emit("dve", v, esem, dsem)

        @block.gpsimd
        def _(g):
            P.emit("pool", g, esem, dsem)

        @block.sync
        def _(s):
            P.emit("sp", s, esem, dsem)
    es.close()
    return nc


def layernorm(P, y, Ry, st, Rst, lnA, lnB, Rln, so):
    for q in range(4):
        P.op("dve", lambda e, o_=st[:, so + q * 6:so + q * 6 + 6], i_=y[:, q * 512:(q + 1) * 512]: e.bn_stats(o_, i_), [Ry], [Rst])
    mv = st[:, so + 24:so + 26]
    P.op("dve", lambda e: e.bn_aggr(mv, st[:, so:so + 24]), [Rst], [Rst])
    P.op("dve", lambda e: e.tensor_scalar(out=st[:, so + 26:so + 27], in0=st[:, so + 25:so + 26], scalar1=LN_EPS, scalar2=None, op0=ALU.add), [Rst], [Rst])
    P.op("act", lambda e: e.activation(out=st[:, so + 27:so + 28], in_=st[:, so + 26:so + 27], func=AF.Ln), [Rst], [Rst])
    P.op("act", lambda e: e.activation(out=st[:, so + 28:so + 29], in_=st[:, so + 27:so + 28], func=AF.Exp, scale=-0.5), [Rst], [Rst])
    P.op("dve", lambda e: e.scalar_tensor_tensor(out=st[:, so + 29:so + 30], in0=st[:, so + 24:so + 25], scalar=-1.0, in1=st[:, so + 28:so + 29], op0=ALU.mult, op1=ALU.mult),
         [Rst], [Rst])
    P.op("act", lambda e: e.activation(out=y, in_=y, func=AF.Identity, scale=st[:, so + 28:so + 29], bias=st[:, so + 29:so + 30]), [Ry, Rst], [Ry])
    P.op("pool", lambda e: e.tensor_tensor(out=y, in0=y, in1=lnA, op=ALU.mult), [Ry, Rln], [Ry])
    P.op("pool", lambda e: e.tensor_tensor(out=y, in0=y, in1=lnB, op=ALU.add), [Ry, Rln], [Ry])


def make_inputs(seqs, SEGM, w_in, gla_w_lr2, gla_b_lr, gla_norm_g, hg_norm_g, lower_bounds, w_out,
                ln1_g, ln1_b, w_up, w_down, ln2_g, ln2_b):
    f32 = np.float32
    common = {
        "w_in": np.ascontiguousarray(w_in, f32), "w_out": np.ascontiguousarray(w_out, f32),
        "w_up": np.ascontiguousarray(w_up, f32), "w_down": np.ascontiguousarray(w_down, f32),
    }
    w2pad = np.zeros((32, L, 2, 512), f32)
    for l in range(L):
        w2pad[0:16, l, 0, :] = gla_w_lr2[l, 0]
        w2pad[16:32, l, 1, :] = gla_w_lr2[l, 1]
    common["w2pad"] = w2pad.reshape(32, L * 2 * 512)
    common["blr"] = np.ascontiguousarray(np.asarray(gla_b_lr, f32).reshape(L, 2, 4, 128).transpose(3, 0, 1, 2).reshape(128, L * 8))
    common["lbp"] = np.ascontiguousarray(np.asarray(lower_bounds, f32).reshape(2, L, 8, 128).transpose(3, 0, 1, 2).reshape(128, 32))
    grow = np.concatenate([np.tile(np.asarray(gla_norm_g, f32), (1, 4)), np.tile(np.asarray(hg_norm_g, f32), (1, 8))], axis=1)
    common["gainrep"] = np.ascontiguousarray(np.broadcast_to(grow[:, None, :], (L, 128, D)))
    for k, v in (("ln1g", ln1_g), ("ln1b", ln1_b), ("ln2g", ln2_g), ("ln2b", ln2_b)):
        common[k] = np.ascontiguousarray(np.broadcast_to(np.asarray(v, f32)[:, None, :], (L, 128, D)))
    common["ident"] = np.eye(128, dtype=f32)
    jj, ii = np.meshgrid(np.arange(128), np.arange(128), indexing="ij")
    common["maskf"] = (jj <= ii).astype(f32)
    common["maskb"] = (jj >= ii).astype(f32)
    sm = np.ones((128, 512), f32)
    sm[:, 0::128] = 0.0
    common["scanmask"] = sm
    maps = []
    for c in range(8):
        parts = []
        for s, sq in enumerate(seqs):
            n = SEGM[s] * 512
            parts.append(np.asarray(sq[c * n:(c + 1) * n], f32))
        m = dict(common)
        m["x"] = np.ascontiguousarray(np.concatenate(parts, axis=0))
        oh = np.zeros((128, 8), f32)
        oh[:, c] = 1.0
        m["onehot"] = oh
        maps.append(m)
    return maps


def run(seqs, SEGM, **params):
    nc = build(SEGM)
    maps = make_inputs(seqs, SEGM, **params)
    res = run_bass_kernel_spmd(nc, maps, core_ids=list(range(8)))
    outs = []
    off = 0
    for s, sq in enumerate(seqs):
        n = SEGM[s] * 512
        outs.append(np.concatenate([res.results[c]["y"][off:off + n] for c in range(8)], axis=0))
        off += n
    return outs


def kernel(x_prompt, x_sample, w_in, gla_w_lr2, gla_b_lr, gla_norm_g, hg_norm_g, lower_bounds, w_out,
           ln1_g, ln1_b, w_up, w_down, ln2_g, ln2_b):
    x_prompt = np.asarray(x_prompt)
    x_sample = np.asarray(x_sample)
    seqs = [x_prompt[0]] + [x_sample[b] for b in range(x_sample.shape[0])]
    outs = run(seqs, FULL_SEGM, w_in=np.asarray(w_in), gla_w_lr2=np.asarray(gla_w_lr2), gla_b_lr=np.asarray(gla_b_lr),
               gla_norm_g=np.asarray(gla_norm_g), hg_norm_g=np.asarray(hg_norm_g), lower_bounds=np.asarray(lower_bounds),
               w_out=np.asarray(w_out), ln1_g=np.asarray(ln1_g), ln1_b=np.asarray(ln1_b), w_up=np.asarray(w_up),
               w_down=np.asarray(w_down), ln2_g=np.asarray(ln2_g), ln2_b=np.asarray(ln2_b))
    y_prompt = outs[0][None].astype(np.float32)
    y_sample = np.stack(outs[1:], axis=0).astype(np.float32)
    return (y_prompt, y_sample)
```

```python
import numpy as np
from contextlib import ExitStack
import concourse.bass as bass
import concourse.mybir as mybir
from concourse.bass_utils import run_bass_kernel_spmd

F32 = mybir.dt.float32
BF16 = mybir.dt.bfloat16
AF = mybir.ActivationFunctionType
ALU = mybir.AluOpType
AX = mybir.AxisListType

D = 2048
DIN = 8224
DFF = 8192
L = 2
ALPHA = (2.0 * L) ** 0.25
LN_EPS = 1e-5
RMS_EPS = 1e-6
DK = 128
C_GQ, C_GK, C_GV, C_GOG, C_GLR, C_HQ, C_HFF, C_HFB, C_HI, C_HOG = 0, 512, 1024, 2048, 3072, 3104, 4128, 5152, 6176, 7200
NBLK = 55
DBG_OUT = ("opf", "vbuf", "gbuf", "slf", "slb", "slend", "h1buf")
FULL_SEGM = [4, 2, 2, 2, 2]
SAME_DIST = 6


class R:
    __slots__ = ("n", "w", "rs", "excl")

    def __init__(self, n, excl=False):
        self.n = n
        self.w = None
        self.rs = []
        self.excl = excl


class O:
    __slots__ = ("e", "fn", "deps", "sig", "val", "i", "dma", "inc")


class Prog:
    ENG = ("pe", "act", "dve", "pool", "sp")

    def __init__(self):
        self.st = {k: [] for k in self.ENG}
        self.bar = {k: None for k in self.ENG}
        self.dma_last = {}

    def op(self, e, fn, rd=(), wr=(), dma=None, inc=16):
        o = O()
        o.e, o.fn, o.dma, o.inc, o.sig, o.val = e, fn, dma, inc, False, 0
        ex = [r for r in rd if r.excl]
        if ex:
            rd = [r for r in rd if not r.excl]
            wr = list(wr) + [r for r in ex if r not in wr]
        d = []
        for r in rd:
            if r.w is not None:
                d.append(r.w)
        for w in wr:
            if w.w is not None:
                d.append(w.w)
            d.extend(w.rs)
        if self.bar[e] is not None:
            d.extend(self.bar[e])
            self.bar[e] = None
        o.deps = self._reduce(d)
        for r in rd:
            r.rs.append(o)
        for w in wr:
            w.w = o
            w.rs = []
        o.i = len(self.st[e])
        self.st[e].append(o)
        if dma is not None:
            self.dma_last[dma] = o
        return o

    @staticmethod
    def _reduce(d):
        best = {}
        for p in d:
            k = ("d", p.dma) if p.dma is not None else ("e", p.e)
            q = best.get(k)
            if q is None or (p.i > q.i if p.dma is None else p.i > q.i):
                best[k] = p
        return list(best.values())

    def barrier(self):
        last = []
        for k in self.ENG:
            for o in reversed(self.st[k]):
                if o.dma is None:
                    last.append(o)
                    break
        last.extend(self.dma_last.values())
        for k in self.ENG:
            self.bar[k] = list(last)

    def finalize(self):
        for e, lst in self.st.items():
            for o in lst:
                keep = []
                for p in o.deps:
                    if p.dma is None and p.e == e and (e == "pe" or (e != "pool" and o.i - p.i > SAME_DIST)):
                        continue
                    if p.dma is None:
                        p.sig = True
                    keep.append(p)
                o.deps = keep
        cnt = {}
        for e, lst in self.st.items():
            c = 0
            for o in lst:
                if o.dma is not None:
                    cnt[o.dma] = cnt.get(o.dma, 0) + o.inc
                    o.val = cnt[o.dma]
                elif o.sig:
                    c += 1
                    o.val = c
        self.dma_tot = cnt

    def emit(self, e, eng, esem, dsem):
        waited = {}
        for o in self.st[e]:
            need = {}
            for p in o.deps:
                k = ("d", p.dma) if p.dma is not None else ("e", p.e)
                if need.get(k, 0) < p.val:
                    need[k] = p.val
            for k, v in need.items():
                if waited.get(k, 0) < v:
                    eng.wait_ge(dsem[k[1]] if k[0] == "d" else esem[k[1]], v)
                    waited[k] = v
            ins = o.fn(eng)
            if o.dma is not None:
                ins.then_inc(dsem[o.dma], o.inc)
            elif o.sig:
                ins.then_inc(esem[e], 1)
        if e == "sp":
            for k, v in self.dma_tot.items():
                if waited.get(("d", k), 0) < v:
                    eng.wait_ge(dsem[k], v)


class Rot:
    def __init__(self, items):
        self.items = list(items)
        self.i = 0

    def next(self):
        x = self.items[self.i % len(self.items)]
        self.i += 1
        return x


def block_pieces():
    pcs = {}
    for b in range(2):
        pl = []
        for i in range(2):
            h = 2 * b + i
            pl.append(("w_in", 0, C_GQ + h * 128, 128, i * 256))
            pl.append(("w_in", 0, C_GK + h * 128, 128, i * 256 + 128))
        pcs[b] = pl
    for h in range(8):
        pcs[2 + h] = [("w_in", 0, C_HQ + h * 128, 128, 0), ("w_in", 0, C_HFF + h * 128, 128, 128),
                      ("w_in", 0, C_HFB + h * 128, 128, 256)]
    pcs[10] = [("w_in", 0, C_GLR, 32, 0)]
    for i, c in enumerate((C_GV, C_GV + 512, C_HI, C_HI + 512, C_GOG, C_GOG + 512, C_HOG, C_HOG + 512)):
        pcs[11 + i] = [("w_in", 0, c, 512, 0)]
    for cb in range(4):
        pcs[19 + cb] = [("w_out", 0, cb * 512, 512, 0)]
    for ub in range(16):
        pcs[23 + ub] = [("w_up", 0, ub * 512, 512, 0)]
    for cb in range(4):
        for g in range(4):
            pcs[39 + cb * 4 + g] = [("w_down", g * 2048, cb * 512, 512, 0)]
    return pcs


def blk_group(b):
    return 0 if b < 19 else (1 if b < 23 else (2 if b < 39 else 3))


def build(SEGM, NL=L, dbg=False):
    nseg = len(SEGM)
    NM = sum(SEGM)
    NT = 4 * NM
    NTOK = NT * 128
    NSD = 2 * nseg
    seg_m0 = [sum(SEGM[:s]) for s in range(nseg)]

    nc = bass.Bass("TRN2", target_bir_lowering=False)

    def din(name, shape, dt=F32):
        return nc.dram_tensor(name, list(shape), dt, kind="ExternalInput").ap()

    def dint(name, shape, dt=F32):
        return nc.dram_tensor(name, list(shape), dt, kind=("ExternalOutput" if (dbg and name in DBG_OUT) else "Internal")).ap()

    x_d = din("x", [NTOK, D])
    wsrc = {"w_in": din("w_in", [NL, D, DIN]), "w_out": din("w_out", [NL, D, D]),
            "w_up": din("w_up", [NL, D, DFF]), "w_down": din("w_down", [NL, DFF, D])}
    w2pad_d = din("w2pad", [32, L * 2 * 512])
    blr_d = din("blr", [128, L * 8])
    lbp_d = din("lbp", [128, 32])
    gain_d = din("gainrep", [L, 128, D])
    ln_d = {k: din(k, [L, 128, D]) for k in ("ln1g", "ln1b", "ln2g", "ln2b")}
    ident_d = din("ident", [128, 128])
    maskf_d = din("maskf", [128, 128])
    maskb_d = din("maskb", [128, 128])
    scanm_d = din("scanmask", [128, 512])
    onehot_d = din("onehot", [128, 8])
    y_d = nc.dram_tensor("y", [NTOK, D], F32, kind="ExternalOutput").ap()

    wbf_d = dint("wbf", [L * NBLK, 128, 8192], BF16)
    hbuf_d = dint("hbuf", [NTOK, D])
    h1buf_d = dint("h1buf", [NTOK, D])
    opf_d = dint("opf", [NT, 128, 24 * 384], BF16)
    vbuf_d = dint("vbuf", [NT, 128, D], BF16)
    gbuf_d = dint("gbuf", [NT, 128, D])
    slf_d = dint("slf", [NT, 128, D])
    slb_d = dint("slb", [NT, 128, D])
    slend_d = dint("slend", [NM, 128, D])
    gin_d = dint("gin", [NSD * 128, 2064])
    gout_d = dint("gout", [8 * NSD * 128, 2064])

    P = Prog()
    Rhbuf = R("hbuf")
    Rh1buf = R("h1buf")
    es = ExitStack()
    ARENA = 53000
    A = es.enter_context(nc.sbuf_tensor("arena", [128, ARENA], F32))
    pb = [es.enter_context(nc.psum_tensor(f"pb{i}", [128, 512], F32)) for i in range(6)]
    pt = [es.enter_context(nc.psum_tensor(f"pt{i}", [128, 8, 128], BF16)) for i in range(2)]
    Rpb = [R(f"pb{i}", True) for i in range(6)]
    Rpt = [R(f"pt{i}", True) for i in range(2)]

    class Arena:
        def __init__(self):
            self.off = 0

        def f(self, n):
            a = A[:, self.off:self.off + n]
            self.off += n
            assert self.off <= ARENA, ("arena overflow", self.off)
            return a

        def b(self, n):
            assert n % 2 == 0
            a = A[:, self.off:self.off + n // 2].bitcast(BF16)
            self.off += n // 2
            assert self.off <= ARENA, ("arena overflow", self.off)
            return a

    ar = Arena()
    identb = ar.b(128); Ridb = R("identb")
    maskfb = ar.b(128); maskbb = ar.b(128); Rmask = R("mask")
    scanm = ar.f(512); Rscanm = R("scanm")
    onehot = ar.f(8); Roh = R("onehot")
    w2b = ar.b(L * 2 * 512); Rw2b = R("w2b")
    nb = ar.f(L * 8); Rnb = R("nb")
    LB = ar.f(32); OML = ar.f(32); NOML = ar.f(32); Rlb = R("lb")
    DwF = ar.f(NT * 12); DcB = ar.f(NT * 12); AmF = ar.f(NM * 12); LmF = ar.f(NM * 12)
    Rtab = R("tables")
    tot = ar.f(96); Rtot = [R(f"tot{i}") for i in range(24)]
    runb = ar.f(12); Rrunb = R("runb")
    sml = ar.f(256); Rsml = R("sml")
    atile = ar.f(12); Ratile = [R("atile0"), R("atile1"), R("atile2")]
    PBASE = ar.off

    dkeys = []

    def dk(name):
        dkeys.append(name)
        return name

    def lbidx(l, d, h):
        return (l * 2 + d) * 8 + h

    pcs = block_pieces()
    Rwg = [[R(f"wg{l}_{g}") for g in range(4)] for l in range(L)]
    conv_list = [(l, b) for l in range(NL) for b in range(NBLK)]
    conv_pos = [0]
    conv_done = set()

    def emit_conv(k):
        while k > 0 and conv_pos[0] < len(conv_list):
            (l, b) = conv_list[conv_pos[0]]
            conv_pos[0] += 1
            k -= 1
            g = blk_group(b)
            key = f"wcv{l}_{g}"
            if key not in dkeys:
                dk(key)
            if b == 10:
                dst3 = wbf_d[l * NBLK + b][:, 0:512].rearrange("p (kc c) -> p kc c", c=32)
            else:
                dst3 = wbf_d[l * NBLK + b].rearrange("p (kc c) -> p kc c", c=512)
            for (tn, r0, c0, n, off) in pcs[b]:
                src = wsrc[tn][l, r0:r0 + 2048, c0:c0 + n].rearrange("(kc p) c -> p kc c", p=128)
                o = P.op("pool", lambda e, d_=dst3[:, :, off:off + n], s_=src: e.dma_start(out=d_, in_=s_), dma=key)
                Rwg[l][g].w = o
            conv_done.add((l, b))

    def conv_tick(l, phase):
        if l == 0 and phase == 1:
            emit_conv(-(-36 // NM))
        elif l == 0 and phase in (2, 3) and NL > 1:
            if conv_pos[0] < NBLK:
                emit_conv(NBLK - conv_pos[0])
            emit_conv(-(-NBLK // (2 * NM)))

    emit_conv(19)

    ar.off = PBASE
    tmpf = ar.f(4096); Rtmpf = R("tmpf")
    k_c = dk("const")

    def ld_const(dst, src, rs, eng="sp"):
        P.op(eng, lambda e, d_=dst, s_=src: e.dma_start(out=d_, in_=s_), wr=rs, dma=k_c)

    ld_const(tmpf[:, 0:128], ident_d[:, :], [Rtmpf])
    ld_const(tmpf[:, 128:256], maskf_d[:, :], [Rtmpf])
    ld_const(tmpf[:, 256:384], maskb_d[:, :], [Rtmpf])
    ld_const(scanm, scanm_d[:, :], [Rscanm])
    ld_const(onehot, onehot_d[:, :], [Roh])
    ld_const(tmpf[0:32, 384:384 + 2048], w2pad_d[:, :], [Rtmpf])
    ld_const(nb, blr_d[:, :], [Rnb])
    ld_const(sml[:, 0:32], lbp_d[:, :], [Rsml])
    P.barrier()
    P.op("dve", lambda e: e.tensor_copy(identb, tmpf[:, 0:128]), [Rtmpf], [Ridb])
    P.op("dve", lambda e: e.tensor_copy(maskfb, tmpf[:, 128:256]), [Rtmpf], [Rmask])
    P.op("dve", lambda e: e.tensor_copy(maskbb, tmpf[:, 256:384]), [Rtmpf], [Rmask])
    P.op("dve", lambda e: e.tensor_copy(w2b[0:32, :], tmpf[0:32, 384:384 + 2048]), [Rtmpf], [Rw2b])
    P.op("dve", lambda e: e.tensor_scalar(out=nb, in0=nb, scalar1=-1.0, scalar2=None, op0=ALU.mult), [Rnb], [Rnb])
    P.op("act", lambda e: e.activation(out=sml[:, 32:64], in_=sml[:, 0:32], func=AF.Exp), [Rsml], [Rsml])
    for d in range(2):
        e0 = sml[:, 32 + d * 16: 32 + d * 16 + 8]
        e1 = sml[:, 32 + d * 16 + 8: 32 + d * 16 + 16]
        s_ = sml[:, 64 + d * 8: 64 + d * 8 + 8]
        r_ = sml[:, 80 + d * 8: 80 + d * 8 + 8]
        P.op("dve", lambda e, a=e0, b=e1, o_=s_: e.tensor_tensor(out=o_, in0=a, in1=b, op=ALU.add), [Rsml], [Rsml])
        P.op("dve", lambda e, a=s_, o_=r_: e.reciprocal(o_, a), [Rsml], [Rsml])
        P.op("dve", lambda e, o_=LB[:, lbidx(0, d, 0):lbidx(0, d, 0) + 8]: e.memset(o_, 0.0), [], [Rlb])
        P.op("dve", lambda e, a=e1, b=r_, o_=LB[:, lbidx(1, d, 0):lbidx(1, d, 0) + 8]: e.tensor_tensor(out=o_, in0=a, in1=b, op=ALU.mult),
             [Rsml], [Rlb])
    P.op("dve", lambda e: e.tensor_scalar(out=OML, in0=LB, scalar1=-1.0, scalar2=1.0, op0=ALU.mult, op1=ALU.add), [Rlb], [Rlb])
    P.op("dve", lambda e: e.tensor_scalar(out=NOML, in0=LB, scalar1=1.0, scalar2=-1.0, op0=ALU.mult, op1=ALU.add), [Rlb], [Rlb])
    P.barrier()

    class Ring:
        def __init__(self, slots, name):
            self.slots = slots
            self.rs = [R(f"{name}{i}") for i in range(len(slots))]
            self.keys = [dk(f"{name}{i}") for i in range(len(slots))]
            self.seq = []
            self.em = 0
            self.us = 0

        def extend(self, seq):
            self.seq.extend(seq)

        def _emit(self):
            (l, b) = self.seq[self.em]
            assert (l, b) in conv_done, ("weight block used before conversion emitted", l, b)
            i = self.em % len(self.slots)
            ncol = 384 if 2 <= b < 10 else 512
            src = wbf_d[l * NBLK + b].rearrange("p (kc c) -> p kc c", c=512)[:, :, 0:ncol]
            dst = self.slots[i][:, :, 0:ncol]
            P.op("sp", lambda e, d_=dst, s_=src: e.dma_start(out=d_, in_=s_), [Rwg[l][blk_group(b)]], [self.rs[i]], dma=self.keys[i])
            self.em += 1

        def next(self, expect):
            assert self.seq[self.us] == expect, (self.seq[self.us], expect)
            while self.em < len(self.seq) and self.em < self.us + len(self.slots):
                self._emit()
            i = self.us % len(self.slots)
            self.us += 1
            return self.slots[i], self.rs[i]

    def mmgroup(out, pairs, rd, wr):
        n = len(pairs)

        def fn(e, out=out, pairs=pairs, n=n):
            ins = None
            for i, (a, b) in enumerate(pairs):
                ins = e.matmul(out, a, b, start=(i == 0), stop=(i == n - 1))
            return ins
        P.op("pe", fn, rd, wr)

    def v3(ap, t=128):
        return ap.rearrange("p (j t) -> p j t", t=t)

    def tile_of(s, m, j):
        return (seg_m0[s] + m) * 4 + j

    def hcols(is_hg, h):
        return (1024 + h * 128, 128) if is_hg else (h * 256, 256)

    def hc12(hh):
        return hcols(hh >= 4, hh - 4 if hh >= 4 else hh)

    def layer_pass1(l):
        in_d = x_d if l == 0 else hbuf_d

        ar.off = PBASE
        gain = ar.f(2048); Rgain = R("gain")
        xT = ar.b(16 * 512).rearrange("p (k c) -> p k c", c=512); RxT = [R(f"xT{j}") for j in range(4)]
        xb = [ar.b(2048) for _ in range(2)]; Rxb = [R("xb0"), R("xb1")]; kxb = [dk(f"p1xb_{i}") for i in range(2)]
        ring = Ring([ar.b(16 * 512).rearrange("p (k c) -> p k c", c=512) for _ in range(2)], "p1w_")
        wglr = ar.b(16 * 32).rearrange("p (k c) -> p k c", c=32); Rwglr = R("wglr"); kwglr = dk(f"p1wglr_")
        glrT = ar.b(512); RglrT = R("glrT")
        qs = [ar.f(512) for _ in range(2)]; Rqs = [R("qs0"), R("qs1")]
        ksA = [ar.f(512) for _ in range(2)]; RksA = [R("ksA0"), R("ksA1")]
        ksB = [ar.f(512) for _ in range(2)]; RksB = [R("ksB0"), R("ksB1")]
        gts = [[ar.f(512) for _ in range(3)] for _ in range(2)]; Rgts = [[R(f"gt{p_}_{i}") for i in range(3)] for p_ in range(2)]
        Lb = [ar.f(512) for _ in range(2)]; RLb = [R("Lb0"), R("Lb1")]
        scs = [[ar.f(512) for _ in range(4)] for _ in range(2)]; Rscs = [[R(f"sc{p_}_{i}") for i in range(4)] for p_ in range(2)]
        Es = [[ar.f(512) for _ in range(4)] for _ in range(2)]; REs = [[R(f"E{p_}_{i}") for i in range(4)] for p_ in range(2)]
        stage = [ar.b(4 * 384) for _ in range(2)]; Rstage = [[R(f"stage{p_}_{i}") for i in range(3)] for p_ in range(2)]; kstage = [dk(f"p1st_{i}") for i in range(2)]
        k3f = [ar.b(512) for _ in range(3)]; Rk3f = [R("k3f0"), R("k3f1"), R("k3f2")]
        k3T = [ar.b(512) for _ in range(2)]; Rk3T = [R("k3T0"), R("k3T1")]
        vt = [ar.b(2048) for _ in range(4)]; Rvt = [R(f"vt{j}") for j in range(4)]; kvt = [dk(f"p1vt_{j}") for j in range(4)]
        silt = [ar.f(512) for _ in range(3)]; Rsilt = [R("silt0"), R("silt1"), R("silt2")]; kGst = [dk(f"p1G_{i}") for i in range(3)]
        Sstf = ar.f(1024); Sstb = ar.f(1024); RSstf = R("Sstf"); RSstb = R("Sstb"); kSst = [dk(f"p1Sf_"), dk(f"p1Sb_")]
        Sb = ar.f(2048); RSb = [R(f"Sb{h}") for h in range(12)]
        Send = ar.f(2048); RSend = R("Send"); kSend = dk(f"p1Send_")
        ginS = ar.f(2064); RginS = R("ginS"); kgin = dk(f"gin_")
        k_misc = dk(f"p1misc_")

        P.op("sp", lambda e, g_=gain, s_=gain_d[l]: e.dma_start(out=g_, in_=s_), [], [Rgain], dma=k_misc)

        pbrot = Rot([0, 1, 2, 3])
        tmrot = Rot([4, 5])
        ptrot = Rot([0, 1])
        parity = [0]

        seqm = [(l, b) for b in (11, 12, 13, 14, 0, 1, 2, 3, 4, 5, 6, 7, 8, 9, 15, 16, 17, 18)]
        ring.extend(seqm * NM)

        def scanprep(n, hd, isb, qA, Rq, kA, Rk, gA, Rg, sc, gt0, is_hg, h):
            par = n % 2
            c, d0, d1, d2 = scs[par]
            Rsc = Rscs[par]
            E, RE = Es[par], REs[par]
            c3, d03, d13, d23 = v3(c), v3(d0), v3(d1), v3(d2)
            mid = c3[:, :, 63:64].to_broadcast([128, 4, 128])
            last = c3[:, :, 127:128].to_broadcast([128, 4, 128])
            sp_ = par
            ap_ = n % 3
            kp_ = n % 3
            at = atile[:, ap_ * 4:ap_ * 4 + 4]
            c0, dv = hcols(is_hg, h)
            hh = (4 + h) if is_hg else h

            def stA():
                P.op("dve", lambda e: e.tensor_tensor_scan(c, scanm, gA, 0.0, ALU.mult, ALU.add), [Rg, Rscanm], [Rsc[0]])
                if not isb:
                    P.op("dve", lambda e: e.tensor_tensor(out=d13, in0=c3, in1=mid, op=ALU.subtract), [Rsc[0]], [Rsc[2]])
                    P.op("dve", lambda e: e.tensor_tensor(out=d23, in0=c3, in1=last, op=ALU.subtract), [Rsc[0]], [Rsc[3]])
                else:
                    P.op("dve", lambda e: e.tensor_tensor(out=d0, in0=c, in1=gA, op=ALU.subtract), [Rsc[0], Rg], [Rsc[1]])
                    P.op("dve", lambda e: e.tensor_tensor(out=d13, in0=d03, in1=last, op=ALU.subtract), [Rsc[0], Rsc[1]], [Rsc[2]])
                    P.op("dve", lambda e: e.tensor_tensor(out=d23, in0=d03, in1=mid, op=ALU.subtract), [Rsc[0], Rsc[1]], [Rsc[3]])
                P.op("dve", lambda e, o_=tot[:, hd * 4:hd * 4 + 4], i_=c3[:, :, 127]: e.tensor_scalar(out=o_, in0=i_, scalar1=float(sc), scalar2=None, op0=ALU.mult),
                     [Rsc[0]], [Rtot[hd]])

            def stB1():
                if not isb:
                    exps = [(c, Rsc[0], sc), (d1, Rsc[2], sc), (d1, Rsc[2], -sc), (d2, Rsc[3], -sc)]
                else:
                    exps = [(d1, Rsc[2], -sc), (d2, Rsc[3], -sc), (d2, Rsc[3], sc), (d0, Rsc[1], sc)]
                for i, (src, rs, s_) in enumerate(exps):
                    P.op("act", lambda e, o_=E[i], i_=src, s_=s_: e.activation(out=o_, in_=i_, func=AF.Exp, scale=float(s_)), [rs], [RE[i]])
                P.op("act", lambda e, o_=at, i_=tot[:, hd * 4:hd * 4 + 4]: e.activation(out=o_, in_=i_, func=AF.Exp), [Rtot[hd]], [Ratile[ap_]])
                stg = stage[sp_].rearrange("p (j o t) -> p j o t", o=3, t=128)
                srcs = [(qA, Rq), (qA, Rq), (kA, Rk)]
                for i in range(3):
                    eng_ = "pool" if i != 1 else "dve"
                    P.op(eng_, lambda e, o_=stg[:, :, i, :], a=v3(srcs[i][0]), b=v3(E[i]): e.tensor_tensor(out=o_, in0=a, in1=b, op=ALU.mult),
                         [srcs[i][1], RE[i]], [Rstage[sp_][i]])
                P.op("dve", lambda e, o_=k3f[kp_], a=kA, b=E[3]: e.tensor_tensor(out=o_, in0=a, in1=b, op=ALU.mult), [Rk, RE[3]], [Rk3f[kp_]])
                dst = opf_d[gt0:gt0 + 4, :, hd * 384:(hd + 1) * 384].rearrange("j p x -> p j x")
                P.op("sp", lambda e, d_=dst, s_=stage[sp_].rearrange("p (j x) -> p j x", x=384): e.dma_start(out=d_, in_=s_),
                     Rstage[sp_], [], dma=kstage[sp_])

            def stB2a():
                pi = ptrot.next()
                for j in range(4):
                    P.op("pe", lambda e, o_=pt[pi][:, j, :], i_=k3f[kp_][:, j * 128:(j + 1) * 128]: e.transpose(o_, i_, identb),
                         [Rk3f[kp_], Ridb], [Rpt[pi]])
                P.op("act", lambda e, o_=v3(k3T[sp_]), i_=pt[pi][:, 0:4, :]: e.activation(out=o_, in_=i_, func=AF.Copy), [Rpt[pi]], [Rk3T[sp_]])

            def stB2b():
                if is_hg:
                    bi = tmrot.next()
                    banks = [bi]
                    outs = [pb[bi][:, j * 128:(j + 1) * 128] for j in range(4)]
                else:
                    banks = [4, 5]
                    outs = [pb[4 + j // 2][:, (j % 2) * 256:(j % 2) * 256 + 256] for j in range(4)]
                for j in range(4):
                    P.op("pe", lambda e, o_=outs[j], a=k3T[sp_][:, j * 128:(j + 1) * 128], b=vt[j][:, c0:c0 + dv]: e.matmul(o_, a, b, start=True, stop=True),
                         [Rk3T[sp_], Rvt[j]], [Rpb[banks[0] if is_hg else 4 + j // 2]])
                Rb = [Rpb[b_] for b_ in banks]
                if not isb:
                    S3 = Sstf[:, 0:4 * dv].rearrange("p (j x) -> p j x", x=dv)
                    P.op("dve", lambda e, o_=S3[:, 0, :]: e.memset(o_, 0.0), [], [RSstf])
                    P.op("dve", lambda e, o_=S3[:, 1, :], i_=outs[0]: e.tensor_copy(o_, i_), Rb, [RSstf])
                    for j in (1, 2):
                        P.op("dve", lambda e, o_=S3[:, j + 1, :], a=S3[:, j, :], s_=at[:, j:j + 1], b=outs[j]:
                             e.scalar_tensor_tensor(out=o_, in0=a, scalar=s_, in1=b, op0=ALU.mult, op1=ALU.add), Rb + [Ratile[ap_], RSstf], [RSstf])
                    P.op("dve", lambda e, o_=Send[:, c0:c0 + dv], a=S3[:, 3, :], s_=at[:, 3:4], b=outs[3]:
                         e.scalar_tensor_tensor(out=o_, in0=a, scalar=s_, in1=b, op0=ALU.mult, op1=ALU.add), Rb + [Ratile[ap_], RSstf], [RSend])
                    dstS = slf_d[gt0:gt0 + 4, :, c0:c0 + dv].rearrange("j p x -> p j x")
                    P.op("sp", lambda e, d_=dstS, s_=S3: e.dma_start(out=d_, in_=s_), [RSstf], [], dma=kSst[0])
                else:
                    S3 = Sstb[:, 0:4 * dv].rearrange("p (j x) -> p j x", x=dv)
                    sbh = Sb[:, c0:c0 + dv]
                    for j in (3, 2, 1, 0):
                        P.op("pool", lambda e, o_=S3[:, j, :], i_=sbh: e.tensor_copy(o_, i_), [RSb[hh]], [RSstb])
                        P.op("dve", lambda e, o_=sbh, s_=at[:, j:j + 1], b=outs[j]:
                             e.scalar_tensor_tensor(out=o_, in0=o_, scalar=s_, in1=b, op0=ALU.mult, op1=ALU.add), Rb + [Ratile[ap_], RSb[hh]], [RSb[hh]])
                    dstS = slb_d[gt0:gt0 + 4, :, c0:c0 + dv].rearrange("j p x -> p j x")
                    P.op("sp", lambda e, d_=dstS, s_=S3: e.dma_start(out=d_, in_=s_), [RSstb], [], dma=kSst[1])

            return stA, stB1, stB2a, stB2b

        totf = tot[:, 0:48].rearrange("p (h j) -> p h j", j=4)
        totb = tot[:, 48:96].rearrange("p (h j) -> p h j", j=4)
        pf = sml[:, 128:176].rearrange("p (h j) -> p h j", j=4)
        sf = sml[:, 176:224].rearrange("p (h j) -> p h j", j=4)

        def tables(gm, gt0):
            P.op("dve", lambda e: e.memset(pf[:, :, 0], 0.0), Rtot, [Rsml])
            for j in (1, 2, 3):
                P.op("dve", lambda e, j=j: e.tensor_tensor(out=pf[:, :, j], in0=pf[:, :, j - 1], in1=totf[:, :, j - 1], op=ALU.add), Rtot + [Rsml], [Rsml])
            dwv = DwF[:, gt0 * 12:(gt0 + 4) * 12].rearrange("p (j h) -> p h j", h=12)
            P.op("act", lambda e: e.activation(out=dwv, in_=pf, func=AF.Exp), [Rsml], [Rtab])
            lm = LmF[:, gm * 12:(gm + 1) * 12]
            am = AmF[:, gm * 12:(gm + 1) * 12]
            P.op("dve", lambda e: e.tensor_tensor(out=lm, in0=pf[:, :, 3], in1=totf[:, :, 3], op=ALU.add), Rtot + [Rsml], [Rtab])
            P.op("act", lambda e: e.activation(out=am, in_=lm, func=AF.Exp), [Rtab], [Rtab])
            P.op("dve", lambda e: e.tensor_copy(sf[:, :, 3], runb), [Rrunb, Rsml], [Rsml])
            for j in (2, 1, 0):
                P.op("dve", lambda e, j=j: e.tensor_tensor(out=sf[:, :, j], in0=sf[:, :, j + 1], in1=totb[:, :, j + 1], op=ALU.add), Rtot + [Rsml], [Rsml])
            dcv = DcB[:, gt0 * 12:(gt0 + 4) * 12].rearrange("p (j h) -> p h j", h=12)
            P.op("act", lambda e: e.activation(out=dcv, in_=sf, func=AF.Exp), [Rsml], [Rtab])
            P.op("dve", lambda e: e.tensor_tensor(out=runb, in0=sf[:, :, 0], in1=totb[:, :, 0], op=ALU.add), Rtot + [Rsml], [Rrunb])

        order = [(s, m) for s in range(nseg) for m in reversed(range(SEGM[s]))]
        prefetched = set()

        def xload(gt, xi):
            P.op("pool", lambda e, d_=xb[xi], s_=in_d[gt * 128:(gt + 1) * 128, :]: e.dma_start(out=d_, in_=s_),
                 [Rhbuf] if l > 0 else [], [Rxb[xi]], dma=kxb[xi])

        for oi, (s, m) in enumerate(order):
            if m == SEGM[s] - 1:
                P.op("pool", lambda e: e.memset(Sb, 0.0), [], RSb)
                P.op("dve", lambda e: e.memset(runb, 0.0), [], [Rrunb])
            if True:
                gm = seg_m0[s] + m
                gt0 = gm * 4
                for j in range(4):
                    xi = j % 2
                    if (gt0 + j) not in prefetched:
                        xload(gt0 + j, xi)
                    for half in range(2):
                        pi = half
                        for k in range(8):
                            kc = half * 8 + k
                            P.op("pe", lambda e, o_=pt[pi][:, k, :], i_=xb[xi][:, kc * 128:(kc + 1) * 128]: e.transpose(o_, i_, identb),
                                 [Rxb[xi], Ridb], [Rpt[pi]])
                        if half == 0:
                            P.op("act", lambda e, o_=xT[:, 0:8, j * 128:(j + 1) * 128], i_=pt[pi][:, :, :]: e.activation(out=o_, in_=i_, func=AF.Copy),
                                 [Rpt[pi]], [RxT[j]])
                        else:
                            P.op("dve", lambda e, o_=xT[:, 8:16, j * 128:(j + 1) * 128], i_=pt[pi][:, :, :]: e.tensor_copy(o_, i_),
                                 [Rpt[pi]], [RxT[j]])
                if oi + 1 < len(order):
                    (s2, m2) = order[oi + 1]
                    for j in range(2):
                        gtn = (seg_m0[s2] + m2) * 4 + j
                        xload(gtn, j)
                        prefetched.add(gtn)
                P.op("sp", lambda e, d_=wglr, s_=wbf_d[l * NBLK + 10][:, 0:512].rearrange("p (kc c) -> p kc c", c=32): e.dma_start(out=d_, in_=s_),
                     [Rwg[l][0]], [Rwglr], dma=kwglr)
                for vb in range(4):
                    W, RW = ring.next((l, 11 + vb))
                    for j in range(4):
                        bi = tmrot.next()
                        mmgroup(pb[bi][:, :], [(xT[:, kc, j * 128:(j + 1) * 128], W[:, kc, :]) for kc in range(16)], [RxT[j], RW], [Rpb[bi]])
                        P.op("dve", lambda e, o_=vt[j][:, vb * 512:(vb + 1) * 512], i_=pb[bi][:, :]: e.tensor_copy(o_, i_), [Rpb[bi]], [Rvt[j]])
                for j in range(4):
                    P.op("sp", lambda e, d_=vbuf_d[gt0 + j], s_=vt[j]: e.dma_start(out=d_, in_=s_), [Rvt[j]], [], dma=kvt[j])
                bi = pbrot.next()
                mmgroup(pb[bi][0:32, :], [(wglr[:, kc, :], xT[:, kc, :]) for kc in range(16)], RxT + [Rwglr], [Rpb[bi]])
                P.op("act", lambda e, i_=pb[bi][0:32, :]: e.activation(out=glrT[0:32, :], in_=i_, func=AF.Copy), [Rpb[bi]], [RglrT])
                tasks = []
                wslot = {}

                def gla_s1(n, h, d):
                    hp, i = h // 2, h % 2
                    qi = h % 2
                    par = n % 2
                    gt1 = gts[par][0]
                    if d == 0:
                        if i == 0:
                            wslot[hp] = ring.next((l, hp))
                        W, RW = wslot[hp]
                        bq = pbrot.next()
                        mmgroup(pb[bq][:, :], [(W[:, kc, i * 256:i * 256 + 128], xT[:, kc, :]) for kc in range(16)], RxT + [RW], [Rpb[bq]])
                        P.op("act", lambda e, o_=qs[qi], i_=pb[bq][:, :]: e.activation(out=o_, in_=i_, func=AF.Copy, scale=float(DK ** -0.5)), [Rpb[bq]], [Rqs[qi]])
                        bk = pbrot.next()
                        mmgroup(pb[bk][:, :], [(W[:, kc, i * 256 + 128:i * 256 + 256], xT[:, kc, :]) for kc in range(16)], RxT + [RW], [Rpb[bk]])
                        P.op("act", lambda e, o_=ksA[qi], i_=pb[bk][:, :]: e.activation(out=o_, in_=i_, func=AF.Copy), [Rpb[bk]], [RksA[qi]])
                    bg = pbrot.next()
                    wof = (l * 2 + d) * 512 + h * 128
                    P.op("pe", lambda e, o_=pb[bg][:, :], a=w2b[0:32, wof:wof + 128], b=glrT[0:32, :]: e.matmul(o_, a, b, start=True, stop=True),
                         [Rw2b, RglrT], [Rpb[bg]])
                    nbi = (l * 2 + d) * 4 + h
                    P.op("act", lambda e, i_=pb[bg][:, :], b_=nb[:, nbi:nbi + 1]: e.activation(out=gt1, in_=i_, func=AF.Exp, scale=-1.0, bias=b_),
                         [Rpb[bg], Rnb], [Rgts[par][0]])
                    P.op("act", lambda e, o_=Lb[par]: e.activation(out=o_, in_=gt1, func=AF.Ln, bias=1.0), [Rgts[par][0]], [RLb[par]])

                def hg_s1(n, h, d):
                    qi = h % 2
                    par = n % 2
                    gt1, gt2, gt3 = gts[par]
                    Rg1, Rg2, Rg3 = Rgts[par]
                    if d == 0:
                        wslot[2 + h] = ring.next((l, 2 + h))
                        W, RW = wslot[2 + h]
                        bq = pbrot.next()
                        mmgroup(pb[bq][:, :], [(W[:, kc, 0:128], xT[:, kc, :]) for kc in range(16)], RxT + [RW], [Rpb[bq]])
                        P.op("act", lambda e, o_=qs[qi], i_=pb[bq][:, :]: e.activation(out=o_, in_=i_, func=AF.Copy), [Rpb[bq]], [Rqs[qi]])
                    W, RW = wslot[2 + h]
                    bz = pbrot.next()
                    mmgroup(pb[bz][:, :], [(W[:, kc, 128 + d * 128:256 + d * 128], xT[:, kc, :]) for kc in range(16)], RxT + [RW], [Rpb[bz]])
                    P.op("act", lambda e, i_=pb[bz][:, :]: e.activation(out=gt1, in_=i_, func=AF.Exp, scale=-1.0), [Rpb[bz]], [Rg1])
                    P.op("act", lambda e: e.activation(out=gt2, in_=gt1, func=AF.Ln, bias=1.0), [Rg1], [Rg2])
                    P.op("act", lambda e: e.activation(out=gt3, in_=gt2, func=AF.Exp, scale=-1.0), [Rg2], [Rg3])
                    ix = lbidx(l, d, h)
                    P.op("act", lambda e, o_=Lb[par], s_=OML[:, ix:ix + 1], b_=LB[:, ix:ix + 1]: e.activation(out=o_, in_=gt3, func=AF.Ln, scale=s_, bias=b_),
                         [Rg3, Rlb], [RLb[par]])
                    kbuf, Rkbuf = (ksA[qi], RksA[qi]) if d == 0 else (ksB[qi], RksB[qi])
                    P.op("pool", lambda e, o_=kbuf, s1=NOML[:, ix:ix + 1], s2=OML[:, ix:ix + 1]:
                         e.tensor_scalar(out=o_, in0=gt3, scalar1=s1, scalar2=s2, op0=ALU.mult, op1=ALU.add), [Rg3, Rlb], [Rkbuf])

                n_ = 0
                for h in range(4):
                    for d in range(2):
                        qi = h % 2
                        stA, stB1, stB2a, stB2b = scanprep(n_, d * 12 + h, d == 1, qs[qi], Rqs[qi], ksA[qi], RksA[qi], Lb[n_ % 2], RLb[n_ % 2], -1.0 / 16.0, gt0, False, h)
                        tasks.append((lambda n=n_, h=h, d=d: gla_s1(n, h, d), stA, stB1, stB2a, stB2b))
                        n_ += 1
                for h in range(8):
                    for d in range(2):
                        qi = h % 2
                        kbuf, Rkbuf = (ksA[qi], RksA[qi]) if d == 0 else (ksB[qi], RksB[qi])
                        stA, stB1, stB2a, stB2b = scanprep(n_, d * 12 + 4 + h, d == 1, qs[qi], Rqs[qi], kbuf, Rkbuf, Lb[n_ % 2], RLb[n_ % 2], 1.0, gt0, True, h)
                        tasks.append((lambda n=n_, h=h, d=d: hg_s1(n, h, d), stA, stB1, stB2a, stB2b))
                        n_ += 1
                NTk = len(tasks)
                for it in range(NTk + 3):
                    if 0 <= it - 3 < NTk:
                        tasks[it - 3][3]()
                    if it < NTk:
                        tasks[it][0]()
                        tasks[it][1]()
                    if 0 <= it - 1 < NTk:
                        tasks[it - 1][2]()
                    if 0 <= it - 3 < NTk:
                        tasks[it - 3][4]()
                P.op("sp", lambda e, d_=slend_d[gm], s_=Send: e.dma_start(out=d_, in_=s_), [RSend], [], dma=kSend)
                for gb in range(4):
                    W, RW = ring.next((l, 15 + gb))
                    for j in range(4):
                        bi = tmrot.next()
                        mmgroup(pb[bi][:, :], [(xT[:, kc, j * 128:(j + 1) * 128], W[:, kc, :]) for kc in range(16)], [RxT[j], RW], [Rpb[bi]])
                        gi = (gb * 4 + j) % 3
                        P.op("act", lambda e, o_=silt[gi], i_=pb[bi][:, :]: e.activation(out=o_, in_=i_, func=AF.Silu), [Rpb[bi]], [Rsilt[gi]])
                        P.op("pool", lambda e, o_=silt[gi], b=gain[:, gb * 512:(gb + 1) * 512]: e.tensor_tensor(out=o_, in0=o_, in1=b, op=ALU.mult),
                             [Rsilt[gi], Rgain], [Rsilt[gi]])
                        P.op("sp", lambda e, d_=gbuf_d[gt0 + j][:, gb * 512:(gb + 1) * 512], s_=silt[gi]: e.dma_start(out=d_, in_=s_), [Rsilt[gi]], [], dma=kGst[gi])
                tables(gm, gt0)
                conv_tick(l, 1)
            if m != 0:
                continue
            P.op("dve", lambda e: e.tensor_copy(ginS[:, 0:2048], Sb), RSb, [RginS])
            P.op("act", lambda e: e.activation(out=ginS[:, 2048:2060], in_=runb, func=AF.Exp), [Rrunb], [RginS])
            P.op("dve", lambda e: e.memset(ginS[:, 2060:2064], 0.0), [], [RginS])
            P.op("pool", lambda e, d_=gin_d[(s * 2 + 1) * 128:(s * 2 + 2) * 128, :]: e.dma_start(out=d_, in_=ginS), [RginS], [], dma=kgin)
        P.barrier()

    def layer_exchange(l):
        ar.off = PBASE
        Tst = ar.f(2048); RT = R("Tst")
        slb_ = [ar.f(2048) for _ in range(2)]; Rslb_ = [R("xsl0"), R("xsl1")]; kx = [dk(f"xsl_{i}") for i in range(2)]
        ginS = ar.f(2064); RginS = R("ginS2")
        slog = ar.f(12); Rslog = R("slog")
        kgin = dk("gin_")
        cnt = 0
        for s in range(nseg):
            P.op("dve", lambda e: e.memset(Tst, 0.0), [], [RT])
            P.op("dve", lambda e: e.memset(slog, 0.0), [], [Rslog])
            for m in range(SEGM[s]):
                gm = seg_m0[s] + m
                bi = cnt % 2
                cnt += 1
                P.op("sp", lambda e, d_=slb_[bi], s_=slend_d[gm]: e.dma_start(out=d_, in_=s_), [], [Rslb_[bi]], dma=kx[bi])
                for hh in range(12):
                    c0, dv = hcols(hh >= 4, hh - 4 if hh >= 4 else hh)
                    P.op("dve", lambda e, o_=Tst[:, c0:c0 + dv], s_=AmF[:, gm * 12 + hh:gm * 12 + hh + 1], b=slb_[bi][:, c0:c0 + dv]:
                         e.scalar_tensor_tensor(out=o_, in0=o_, scalar=s_, in1=b, op0=ALU.mult, op1=ALU.add), [Rslb_[bi], Rtab, RT], [RT])
                P.op("dve", lambda e, gm=gm: e.tensor_tensor(out=slog, in0=slog, in1=LmF[:, gm * 12:(gm + 1) * 12], op=ALU.add), [Rtab, Rslog], [Rslog])
            P.op("dve", lambda e: e.tensor_copy(ginS[:, 0:2048], Tst), [RT], [RginS])
            P.op("act", lambda e: e.activation(out=ginS[:, 2048:2060], in_=slog, func=AF.Exp), [Rslog], [RginS])
            P.op("dve", lambda e: e.memset(ginS[:, 2060:2064], 0.0), [], [RginS])
            P.op("pool", lambda e, d_=gin_d[(s * 2) * 128:(s * 2 + 1) * 128, :]: e.dma_start(out=d_, in_=ginS), [RginS], [], dma=kgin)
        P.barrier()
        kcc = dk(f"cc_")
        P.op("pool", lambda e: e.collective_compute("AllGather", ALU.bypass, replica_groups=[list(range(8))], ins=[gin_d[:, :]], outs=[gout_d[:, :]]),
             [], [], dma=kcc, inc=1)
        P.barrier()

    def layer_pass2(l):
        in_d = x_d if l == 0 else hbuf_d
        ar.off = PBASE
        lnA = ar.f(2048); lnB = ar.f(2048); Rln = R("ln"); kln = dk(f"p2ln_")
        opb = ar.b(24 * 384); Ropb = R("opb"); kopb = dk(f"p2op_")
        v2 = [ar.b(2048) for _ in range(2)]; Rv2 = [R("v20"), R("v21")]; kv2 = [dk(f"p2v_{i}") for i in range(2)]
        G2 = ar.f(2048); RG2 = R("G2"); kG2 = dk(f"p2G_")
        SLf = ar.f(2048); SLb = ar.f(2048); RSLf = R("SLf"); RSLb = R("SLb"); kSL = [dk(f"p2SLf_"), dk(f"p2SLb_")]
        Sfm = ar.f(2048); Sinb = ar.f(2048); RSfm = R("Sfm"); RSinb = R("Sinb")
        Sfb = ar.b(2048); Sbb = ar.b(2048); RSfb = [R(f"Sfb{h}") for h in range(12)]; RSbb = [R(f"Sbb{h}") for h in range(12)]
        scTf = [ar.b(128) for _ in range(8)]; scTb = [ar.b(128) for _ in range(8)]
        RscTf = [R(f"scTf{i}") for i in range(8)]; RscTb = [R(f"scTb{i}") for i in range(8)]
        sgc = [0]
        sq = ar.f(2048); Rsq = R("sq")
        of = [ar.b(2048) for _ in range(2)]; Rof = [R("of0"), R("of1")]
        oT = ar.b(16 * 256).rearrange("p (k c) -> p k c", c=256); RoT = [R("oT0"), R("oT1")]
        ring = Ring([ar.b(16 * 512).rearrange("p (k c) -> p k c", c=512) for _ in range(2)], "p2w_")
        xp = [ar.f(512) for _ in range(4)]; Rxp = [R(f"xp{i}") for i in range(4)]; kxp = [dk(f"p2xp_{i}") for i in range(4)]
        y1 = [ar.f(2048) for _ in range(2)]; Ry1 = [R("y10"), R("y11")]; ky1 = [dk(f"p2y_{i}") for i in range(2)]
        gath = ar.f(2064); Rgath = R("gath"); kgath = dk(f"p2ga_")
        Tst = sq; RT = Rsq
        st = ar.f(128); Rst = R("st")
        P.op("sp", lambda e, s_=ln_d["ln1g"][l]: e.dma_start(out=lnA, in_=s_), [], [Rln], dma=kln)
        P.op("sp", lambda e, s_=ln_d["ln1b"][l]: e.dma_start(out=lnB, in_=s_), [], [Rln], dma=kln)
        for i in range(8):
            P.op("dve", lambda e, o_=scTf[i]: e.memset(o_, 0.0), [], [RscTf[i]])
            P.op("dve", lambda e, o_=scTb[i]: e.memset(o_, 0.0), [], [RscTb[i]])
        ring.extend([(l, 19 + cb) for cb in range(4)] * (2 * NM))
        xprot = Rot([0, 1, 2, 3])

        for s in range(nseg):
            for d, (acc, Racc) in enumerate(((Sfm, RSfm), (Sinb, RSinb))):
                P.op("dve", lambda e: e.memset(Tst, 0.0), [], [RT])
                P.op("dve", lambda e, a=acc: e.memset(a, 0.0), [], [Racc])
                order = list(range(8)) if d == 0 else list(range(7, -1, -1))
                for ri, r in enumerate(order):
                    P.op("dve", lambda e, a=acc, s_=onehot[:, r:r + 1]: e.scalar_tensor_tensor(out=a, in0=Tst, scalar=s_, in1=a, op0=ALU.mult, op1=ALU.add),
                         [RT, Racc, Roh], [Racc])
                    if ri == 7:
                        break
                    row = (r * NSD + s * 2 + d) * 128
                    P.op("sp", lambda e, s_=gout_d[row:row + 128, :]: e.dma_start(out=gath, in_=s_), [], [Rgath], dma=kgath)
                    for hh in range(12):
                        c0, dv = hc12(hh)
                        P.op("dve", lambda e, o_=Tst[:, c0:c0 + dv], s_=gath[:, 2048 + hh:2049 + hh], b=gath[:, c0:c0 + dv]:
                             e.scalar_tensor_tensor(out=o_, in0=o_, scalar=s_, in1=b, op0=ALU.mult, op1=ALU.add), [Rgath, RT], [RT])
            for m in range(SEGM[s]):
                gm = seg_m0[s] + m
                for j in range(4):
                    gt = gm * 4 + j
                    jj = j % 2
                    vi = j % 2
                    P.op("sp", lambda e, s_=opf_d[gt]: e.dma_start(out=opb, in_=s_), [], [Ropb], dma=kopb)
                    P.op("sp", lambda e, d_=v2[vi], s_=vbuf_d[gt]: e.dma_start(out=d_, in_=s_), [], [Rv2[vi]], dma=kv2[vi])
                    P.op("sp", lambda e, s_=gbuf_d[gt]: e.dma_start(out=G2, in_=s_), [], [RG2], dma=kG2)
                    P.op("sp", lambda e, s_=slf_d[gt]: e.dma_start(out=SLf, in_=s_), [], [RSLf], dma=kSL[0])
                    P.op("sp", lambda e, s_=slb_d[gt]: e.dma_start(out=SLb, in_=s_), [], [RSLb], dma=kSL[1])
                    for hh in range(12):
                        c0, dv = hc12(hh)
                        P.op("dve", lambda e, o_=Sfb[:, c0:c0 + dv], a=Sfm[:, c0:c0 + dv], s_=DwF[:, gt * 12 + hh:gt * 12 + hh + 1], b=SLf[:, c0:c0 + dv]:
                             e.scalar_tensor_tensor(out=o_, in0=a, scalar=s_, in1=b, op0=ALU.mult, op1=ALU.add), [RSfm, Rtab, RSLf], [RSfb[hh]])
                        P.op("dve", lambda e, o_=Sbb[:, c0:c0 + dv], a=Sinb[:, c0:c0 + dv], s_=DcB[:, gt * 12 + hh:gt * 12 + hh + 1], b=SLb[:, c0:c0 + dv]:
                             e.scalar_tensor_tensor(out=o_, in0=a, scalar=s_, in1=b, op0=ALU.mult, op1=ALU.add), [RSinb, Rtab, RSLb], [RSbb[hh]])
                    for gi, grp in enumerate(([0, 1, 2, 3], [4, 5, 6, 7], [8, 9, 10, 11])):
                        for d in range(2):
                            for qi, hh in enumerate(grp):
                                hd = d * 12 + hh
                                ps = pb[d][:, qi * 128:(qi + 1) * 128]
                                km = opb[:, hd * 384 + 256:hd * 384 + 384]
                                qm = opb[:, hd * 384 + 128:hd * 384 + 256]
                                P.op("pe", lambda e, o_=ps, a=km, b=qm: e.matmul(o_, a, b, start=True, stop=True), [Ropb], [Rpb[d]])
                        scts = {}
                        for d in range(2):
                            for qi, hh in enumerate(grp):
                                ps = pb[d][:, qi * 128:(qi + 1) * 128]
                                si = (sgc[0] % 2) * 4 + qi
                                if d == 0:
                                    t_, Rt_ = scTf[si], RscTf[si]
                                    P.op("dve", lambda e, o_=t_[:, 64:128], a=ps[:, 64:128], b=maskfb[:, 64:128]: e.tensor_tensor(out=o_, in0=a, in1=b, op=ALU.mult),
                                         [Rpb[d], Rmask], [Rt_])
                                    P.op("dve", lambda e, o_=t_[0:64, 0:64], a=ps[0:64, 0:64], b=maskfb[0:64, 0:64]: e.tensor_tensor(out=o_, in0=a, in1=b, op=ALU.mult),
                                         [Rpb[d], Rmask], [Rt_])
                                else:
                                    t_, Rt_ = scTb[si], RscTb[si]
                                    P.op("dve", lambda e, o_=t_[:, 0:64], a=ps[:, 0:64], b=maskbb[:, 0:64]: e.tensor_tensor(out=o_, in0=a, in1=b, op=ALU.mult),
                                         [Rpb[d], Rmask], [Rt_])
                                    P.op("dve", lambda e, o_=t_[64:128, 64:128], a=ps[64:128, 64:128], b=maskbb[64:128, 64:128]: e.tensor_tensor(out=o_, in0=a, in1=b, op=ALU.mult),
                                         [Rpb[d], Rmask], [Rt_])
                                scts[(d, hh)] = (t_, Rt_)
                        sgc[0] += 1
                        for hh in grp:
                            c0, dv = hc12(hh)
                            if hh < 4:
                                ob, oc = 2 + hh // 2, (hh % 2) * 256
                            else:
                                ob, oc = 4 + (hh - 4) // 4, ((hh - 4) % 4) * 128
                            q1f = opb[:, hh * 384:hh * 384 + 128]
                            q1b = opb[:, (12 + hh) * 384:(12 + hh) * 384 + 128]
                            vh = v2[vi][:, c0:c0 + dv]
                            mmgroup(pb[ob][:, oc:oc + dv], [(scts[(0, hh)][0], vh), (scts[(1, hh)][0], vh), (q1f, Sfb[:, c0:c0 + dv]), (q1b, Sbb[:, c0:c0 + dv])],
                                    [scts[(0, hh)][1], scts[(1, hh)][1], Rv2[vi], Ropb, RSfb[hh], RSbb[hh]], [Rpb[ob]])
                    for q4 in range(4):
                        P.op("act", lambda e, o_=sq[:, q4 * 512:(q4 + 1) * 512], i_=pb[2 + q4][:, :]: e.activation(out=o_, in_=i_, func=AF.Square), [Rpb[2 + q4]], [Rsq])
                    P.op("dve", lambda e: e.tensor_reduce(out=st[:, 0:4], in_=sq[:, 0:1024].rearrange("p (h x) -> p h x", x=256), axis=AX.X, op=ALU.add), [Rsq], [Rst])
                    P.op("dve", lambda e: e.tensor_reduce(out=st[:, 4:12], in_=sq[:, 1024:2048].rearrange("p (h x) -> p h x", x=128), axis=AX.X, op=ALU.add), [Rsq], [Rst])
                    P.op("dve", lambda e: e.tensor_scalar(out=st[:, 0:4], in0=st[:, 0:4], scalar1=1.0 / 256.0, scalar2=RMS_EPS, op0=ALU.mult, op1=ALU.add), [Rst], [Rst])
                    P.op("dve", lambda e: e.tensor_scalar(out=st[:, 4:12], in0=st[:, 4:12], scalar1=1.0 / 128.0, scalar2=RMS_EPS, op0=ALU.mult, op1=ALU.add), [Rst], [Rst])
                    P.op("act", lambda e: e.activation(out=st[:, 16:28], in_=st[:, 0:12], func=AF.Ln), [Rst], [Rst])
                    P.op("act", lambda e: e.activation(out=st[:, 32:44], in_=st[:, 16:28], func=AF.Exp, scale=-0.5), [Rst], [Rst])
                    for hh in range(12):
                        c0, dv = hc12(hh)
                        if hh < 4:
                            ob, oc = 2 + hh // 2, (hh % 2) * 256
                        else:
                            ob, oc = 4 + (hh - 4) // 4, ((hh - 4) % 4) * 128
                        P.op("dve", lambda e, o_=of[jj][:, c0:c0 + dv], a=pb[ob][:, oc:oc + dv], s_=st[:, 32 + hh:33 + hh], b=G2[:, c0:c0 + dv]:
                             e.scalar_tensor_tensor(out=o_, in0=a, scalar=s_, in1=b, op0=ALU.mult, op1=ALU.mult), [Rpb[ob], Rst, RG2], [Rof[jj]])
                    for half in range(2):
                        for k in range(8):
                            kc = half * 8 + k
                            P.op("pe", lambda e, o_=pt[half][:, k, :], i_=of[jj][:, kc * 128:(kc + 1) * 128]: e.transpose(o_, i_, identb),
                                 [Rof[jj], Ridb], [Rpt[half]])
                        P.op("act", lambda e, o_=oT[:, half * 8:half * 8 + 8, jj * 128:(jj + 1) * 128], i_=pt[half][:, :, :]: e.activation(out=o_, in_=i_, func=AF.Copy),
                             [Rpt[half]], [RoT[jj]])
                    if jj == 1:
                        for cb in range(4):
                            W, RW = ring.next((l, 19 + cb))
                            for t2 in range(2):
                                gt2 = gm * 4 + (j - 1) + t2
                                xi = xprot.next()
                                P.op("sp", lambda e, d_=xp[xi], s_=in_d[gt2 * 128:(gt2 + 1) * 128, cb * 512:(cb + 1) * 512]: e.dma_start(out=d_, in_=s_),
                                     [Rhbuf] if l > 0 else [], [Rxp[xi]], dma=kxp[xi])
                                bi = t2
                                mmgroup(pb[bi][:, :], [(oT[:, kc, t2 * 128:(t2 + 1) * 128], W[:, kc, :]) for kc in range(16)], [RoT[t2], RW],
                                        [Rpb[bi]])
                                P.op("dve", lambda e, o_=y1[t2][:, cb * 512:(cb + 1) * 512], a=xp[xi], b=pb[bi][:, :]:
                                     e.scalar_tensor_tensor(out=o_, in0=a, scalar=float(ALPHA), in1=b, op0=ALU.mult, op1=ALU.add), [Rxp[xi], Rpb[bi]], [Ry1[t2]])
                        for t2 in range(2):
                            gt2 = gm * 4 + (j - 1) + t2
                            layernorm(P, y1[t2], Ry1[t2], st, Rst, lnA, lnB, Rln, 64)
                            P.op("pool", lambda e, d_=h1buf_d[gt2 * 128:(gt2 + 1) * 128, :], s_=y1[t2]: e.dma_start(out=d_, in_=s_), [Ry1[t2]], [Rh1buf], dma=ky1[t2])
                conv_tick(l, 2)
                P.op("sp", lambda e, s_=slend_d[gm]: e.dma_start(out=SLf, in_=s_), [], [RSLf], dma=kSL[0])
                for hh in range(12):
                    c0, dv = hc12(hh)
                    P.op("dve", lambda e, o_=Sfm[:, c0:c0 + dv], s_=AmF[:, gm * 12 + hh:gm * 12 + hh + 1], b=SLf[:, c0:c0 + dv]:
                         e.scalar_tensor_tensor(out=o_, in0=o_, scalar=s_, in1=b, op0=ALU.mult, op1=ALU.add), [RSfm, Rtab, RSLf], [RSfm])
        P.barrier()

    def layer_pass3(l):
        out_d = y_d if l == NL - 1 else hbuf_d
        ar.off = PBASE
        lnA = ar.f(2048); lnB = ar.f(2048); Rln = R("ln3"); kln = dk(f"p3ln_")
        h1s = [ar.f(2048) for _ in range(4)]; Rh1s = [R(f"h1s{j}") for j in range(4)]; kh1 = [dk(f"p3h_{j}") for j in range(4)]; ko = [dk(f"p3o_{j}") for j in range(4)]
        hb = ar.b(2048); Rhb = R("hb")
        hT = ar.b(16 * 512).rearrange("p (k c) -> p k c", c=512); RhT = [R(f"hT{j}") for j in range(4)]
        uT = ar.b(64 * 512).rearrange("p (k c) -> p k c", c=512); RuT = [R(f"uT{g}") for g in range(4)]
        ring = Ring([ar.b(16 * 512).rearrange("p (k c) -> p k c", c=512) for _ in range(3)], "p3w_")
        rt = [ar.f(512) for _ in range(2)]; Rrt = [R("rt0"), R("rt1")]
        st = ar.f(64); Rst = R("st3")
        P.op("sp", lambda e, s_=ln_d["ln2g"][l]: e.dma_start(out=lnA, in_=s_), [], [Rln], dma=kln)
        P.op("sp", lambda e, s_=ln_d["ln2b"][l]: e.dma_start(out=lnB, in_=s_), [], [Rln], dma=kln)
        ring.extend(([(l, 23 + ub) for ub in range(16)] + [(l, 39 + cb * 4 + g) for cb in range(4) for g in range(4)]) * NM)
        uprot = Rot([0, 1])
        for gm in range(NM):
            conv_tick(l, 3)
            for j in range(4):
                gt = gm * 4 + j
                P.op("sp", lambda e, d_=h1s[j], s_=h1buf_d[gt * 128:(gt + 1) * 128, :]: e.dma_start(out=d_, in_=s_), [Rh1buf], [Rh1s[j]], dma=kh1[j])
                P.op("pool", lambda e, i_=h1s[j]: e.tensor_copy(hb, i_), [Rh1s[j]], [Rhb])
                for half in range(2):
                    for k in range(8):
                        kc = half * 8 + k
                        P.op("pe", lambda e, o_=pt[half][:, k, :], i_=hb[:, kc * 128:(kc + 1) * 128]: e.transpose(o_, i_, identb),
                             [Rhb, Ridb], [Rpt[half]])
                    P.op("act", lambda e, o_=hT[:, half * 8:half * 8 + 8, j * 128:(j + 1) * 128], i_=pt[half][:, :, :]: e.activation(out=o_, in_=i_, func=AF.Copy),
                         [Rpt[half]], [RhT[j]])
            for ub in range(16):
                W, RW = ring.next((l, 23 + ub))
                for q in range(4):
                    fc = ub * 4 + q
                    bi = uprot.next()
                    mmgroup(pb[bi][:, :], [(W[:, kc, q * 128:(q + 1) * 128], hT[:, kc, :]) for kc in range(16)], RhT + [RW], [Rpb[bi]])
                    ri = fc % 2
                    P.op("act", lambda e, o_=rt[ri], i_=pb[bi][:, :]: e.activation(out=o_, in_=i_, func=AF.Relu), [Rpb[bi]], [Rrt[ri]])
                    P.op("pool", lambda e, o_=uT[:, fc, :], a=rt[ri]: e.tensor_tensor(out=o_, in0=a, in1=a, op=ALU.mult), [Rrt[ri]], [RuT[fc // 16]])
            for cb in range(4):
                for g in range(4):
                    W, RW = ring.next((l, 39 + cb * 4 + g))
                    for j in range(4):
                        def fn(e, j=j, g=g, W=W):
                            ins = None
                            for q in range(16):
                                ins = e.matmul(pb[2 + j][:, :], uT[:, g * 16 + q, j * 128:(j + 1) * 128], W[:, q, :],
                                               start=(g == 0 and q == 0), stop=(g == 3 and q == 15))
                            return ins
                        P.op("pe", fn, [RuT[g], RW], [Rpb[2 + j]])
                for j in range(4):
                    P.op("dve", lambda e, o_=h1s[j][:, cb * 512:(cb + 1) * 512], b=pb[2 + j][:, :]:
                         e.scalar_tensor_tensor(out=o_, in0=o_, scalar=float(ALPHA), in1=b, op0=ALU.mult, op1=ALU.add), [Rpb[2 + j], Rh1s[j]], [Rh1s[j]])
            for j in range(4):
                gt = gm * 4 + j
                layernorm(P, h1s[j], Rh1s[j], st, Rst, lnA, lnB, Rln, 0)
                P.op("pool", lambda e, d_=out_d[gt * 128:(gt + 1) * 128, :], s_=h1s[j]: e.dma_start(out=d_, in_=s_), [Rh1s[j]], [Rhbuf], dma=ko[j])
        P.barrier()

    for l in range(NL):
        layer_pass1(l)
        layer_exchange(l)
        layer_pass2(l)
        layer_pass3(l)

    P.finalize()
    esem = {k: es.enter_context(nc.semaphore(f"e_{k}")) for k in Prog.ENG}
    dsem = {k: es.enter_context(nc.semaphore(f"d_{i}")) for i, k in enumerate(dict.fromkeys(dkeys))}
    with nc.Block() as block:
        @block.tensor
        def _(t):
            P.emit("pe", t, esem, dsem)

        @block.scalar
        def _(a):
            P.emit("act", a, esem, dsem)

        @block.vector
        def _(v):
            P.emit("dve", v, esem, dsem)

        @block.gpsimd
        def _(g):
            P.emit("pool", g, esem, dsem)

        @block.sync
        def _(s):
            P.emit("sp", s, esem, dsem)
    es.close()
    return nc


def layernorm(P, y, Ry, st, Rst, lnA, lnB, Rln, so):
    for q in range(4):
        P.op("dve", lambda e, o_=st[:, so + q * 6:so + q * 6 + 6], i_=y[:, q * 512:(q + 1) * 512]: e.bn_stats(o_, i_), [Ry], [Rst])
    mv = st[:, so + 24:so + 26]
    P.op("dve", lambda e: e.bn_aggr(mv, st[:, so:so + 24]), [Rst], [Rst])
    P.op("dve", lambda e: e.tensor_scalar(out=st[:, so + 26:so + 27], in0=st[:, so + 25:so + 26], scalar1=LN_EPS, scalar2=None, op0=ALU.add), [Rst], [Rst])
    P.op("act", lambda e: e.activation(out=st[:, so + 27:so + 28], in_=st[:, so + 26:so + 27], func=AF.Ln), [Rst], [Rst])
    P.op("act", lambda e: e.activation(out=st[:, so + 28:so + 29], in_=st[:, so + 27:so + 28], func=AF.Exp, scale=-0.5), [Rst], [Rst])
    P.op("dve", lambda e: e.scalar_tensor_tensor(out=st[:, so + 29:so + 30], in0=st[:, so + 24:so + 25], scalar=-1.0, in1=st[:, so + 28:so + 29], op0=ALU.mult, op1=ALU.mult),
         [Rst], [Rst])
    P.op("act", lambda e: e.activation(out=y, in_=y, func=AF.Identity, scale=st[:, so + 28:so + 29], bias=st[:, so + 29:so + 30]), [Ry, Rst], [Ry])
    P.op("pool", lambda e: e.tensor_tensor(out=y, in0=y, in1=lnA, op=ALU.mult), [Ry, Rln], [Ry])
    P.op("pool", lambda e: e.tensor_tensor(out=y, in0=y, in1=lnB, op=ALU.add), [Ry, Rln], [Ry])


def make_inputs(seqs, SEGM, w_in, gla_w_lr2, gla_b_lr, gla_norm_g, hg_norm_g, lower_bounds, w_out,
                ln1_g, ln1_b, w_up, w_down, ln2_g, ln2_b):
    f32 = np.float32
    common = {
        "w_in": np.ascontiguousarray(w_in, f32), "w_out": np.ascontiguousarray(w_out, f32),
        "w_up": np.ascontiguousarray(w_up, f32), "w_down": np.ascontiguousarray(w_down, f32),
    }
    w2pad = np.zeros((32, L, 2, 512), f32)
    for l in range(L):
        w2pad[0:16, l, 0, :] = gla_w_lr2[l, 0]
        w2pad[16:32, l, 1, :] = gla_w_lr2[l, 1]
    common["w2pad"] = w2pad.reshape(32, L * 2 * 512)
    common["blr"] = np.ascontiguousarray(np.asarray(gla_b_lr, f32).reshape(L, 2, 4, 128).transpose(3, 0, 1, 2).reshape(128, L * 8))
    common["lbp"] = np.ascontiguousarray(np.asarray(lower_bounds, f32).reshape(2, L, 8, 128).transpose(3, 0, 1, 2).reshape(128, 32))
    grow = np.concatenate([np.tile(np.asarray(gla_norm_g, f32), (1, 4)), np.tile(np.asarray(hg_norm_g, f32), (1, 8))], axis=1)
    common["gainrep"] = np.ascontiguousarray(np.broadcast_to(grow[:, None, :], (L, 128, D)))
    for k, v in (("ln1g", ln1_g), ("ln1b", ln1_b), ("ln2g", ln2_g), ("ln2b", ln2_b)):
        common[k] = np.ascontiguousarray(np.broadcast_to(np.asarray(v, f32)[:, None, :], (L, 128, D)))
    common["ident"] = np.eye(128, dtype=f32)
    jj, ii = np.meshgrid(np.arange(128), np.arange(128), indexing="ij")
    common["maskf"] = (jj <= ii).astype(f32)
    common["maskb"] = (jj >= ii).astype(f32)
    sm = np.ones((128, 512), f32)
    sm[:, 0::128] = 0.0
    common["scanmask"] = sm
    maps = []
    for c in range(8):
        parts = []
        for s, sq in enumerate(seqs):
            n = SEGM[s] * 512
            parts.append(np.asarray(sq[c * n:(c + 1) * n], f32))
        m = dict(common)
        m["x"] = np.ascontiguousarray(np.concatenate(parts, axis=0))
        oh = np.zeros((128, 8), f32)
        oh[:, c] = 1.0
        m["onehot"] = oh
        maps.append(m)
    return maps


def run(seqs, SEGM, **params):
    nc = build(SEGM)
    maps = make_inputs(seqs, SEGM, **params)
    res = run_bass_kernel_spmd(nc, maps, core_ids=list(range(8)))
    outs = []
    off = 0
    for s, sq in enumerate(seqs):
        n = SEGM[s] * 512
        outs.append(np.concatenate([res.results[c]["y"][off:off + n] for c in range(8)], axis=0))
        off += n
    return outs


def kernel(x_prompt, x_sample, w_in, gla_w_lr2, gla_b_lr, gla_norm_g, hg_norm_g, lower_bounds, w_out,
           ln1_g, ln1_b, w_up, w_down, ln2_g, ln2_b):
    x_prompt = np.asarray(x_prompt)
    x_sample = np.asarray(x_sample)
    seqs = [x_prompt[0]] + [x_sample[b] for b in range(x_sample.shape[0])]
    outs = run(seqs, FULL_SEGM, w_in=np.asarray(w_in), gla_w_lr2=np.asarray(gla_w_lr2), gla_b_lr=np.asarray(gla_b_lr),
               gla_norm_g=np.asarray(gla_norm_g), hg_norm_g=np.asarray(hg_norm_g), lower_bounds=np.asarray(lower_bounds),
               w_out=np.asarray(w_out), ln1_g=np.asarray(ln1_g), ln1_b=np.asarray(ln1_b), w_up=np.asarray(w_up),
               w_down=np.asarray(w_down), ln2_g=np.asarray(ln2_g), ln2_b=np.asarray(ln2_b))
    y_prompt = outs[0][None].astype(np.float32)
    y_sample = np.stack(outs[1:], axis=0).astype(np.float32)
    return (y_prompt, y_sample)
```
